# Optimizing a Trainium2 kernel written in Bass

```python
import jax, jax.numpy as jnp
from jax import lax
import numpy as np

D_MODEL = 1024
BATCH = 8
SEQ = 2048
DEPTH = 2

HEAD_DIM = 64
A_GROUPS = 4
A_CHUNK = 128
A_GROUP_DIM = 128
A_WIDTH = A_GROUPS * A_GROUP_DIM
B_PATTERNS = ((128, 1), (512, 4), (2048, 16))
B_GROUPS = len(B_PATTERNS)
B_HEADS_PER_GROUP = 4
B_WIDTH = B_GROUPS * B_HEADS_PER_GROUP * HEAD_DIM
B_OUT = B_HEADS_PER_GROUP * HEAD_DIM
B_QBLOCK = 128
C_HEADS = 8
C_WIDTH = C_HEADS * HEAD_DIM
C_QBLOCK = 128
N_BRANCH = 3
SPLIT_A = 2 * A_WIDTH
SPLIT_B = SPLIT_A + 3 * B_WIDTH
SPLIT_C = SPLIT_B + 3 * C_WIDTH
IN_COLS = SPLIT_C + N_BRANCH * D_MODEL
D_FF = 2816
EPS = 1e-6

kernel_name = "hybrid_gated_gmlp_dilated_stickbreaking_block"


def rms_norm(x, g):
    xf = x.astype(jnp.float32)
    y = xf * lax.rsqrt(jnp.mean(xf * xf, axis=-1, keepdims=True) + EPS)
    return (y * g.astype(jnp.float32)).astype(x.dtype)


def layer_norm(x, g, b):
    xf = x.astype(jnp.float32)
    mu = jnp.mean(xf, axis=-1, keepdims=True)
    xc = xf - mu
    y = xc * lax.rsqrt(jnp.mean(xc * xc, axis=-1, keepdims=True) + EPS)
    return (y * g.astype(jnp.float32) + b.astype(jnp.float32)).astype(x.dtype)


def swiglu(h, wi, wo):
    gate, up = jnp.split(h @ wi, 2, axis=-1)
    return (jax.nn.silu(gate) * up) @ wo


def chunked_spatial_gating(z, ln_g, ln_b, w_s, b_s):
    u, v = jnp.split(z, 2, axis=-1)
    v = layer_norm(v, ln_g, ln_b)
    bsz, s = v.shape[0], v.shape[1]
    n_chunks = s // A_CHUNK
    v = v.reshape(bsz, n_chunks, A_CHUNK, A_GROUPS, A_GROUP_DIM)
    causal = jnp.tril(jnp.ones((A_CHUNK, A_CHUNK), dtype=bool))
    w = jnp.where(causal, w_s, 0)
    sv = jnp.einsum('gts,bcsgd->bctgd', w, v) + b_s.T[:, :, None]
    return u * sv.reshape(bsz, s, A_WIDTH)


def dilated_attention(q, k, v):
    bsz, s = q.shape[0], q.shape[1]
    n_blocks = s // B_QBLOCK
    scale = HEAD_DIM ** -0.5
    q_blocks = q.reshape(bsz, n_blocks, B_QBLOCK, B_GROUPS, B_HEADS_PER_GROUP, HEAD_DIM).swapaxes(0, 1)
    starts = jnp.arange(n_blocks) * B_QBLOCK

    def block(args):
        q_blk, start = args
        t = start + jnp.arange(B_QBLOCK)
        outs, maxes, sums = [], [], []
        for g, (window, dil) in enumerate(B_PATTERNS):
            n_keys = window // dil + 1
            idx = t[:, None] - dil * jnp.arange(n_keys)[None, :]
            valid = idx >= 0
            idx = jnp.maximum(idx, 0)
            kg = k[:, :, g][:, idx]
            vg = v[:, :, g][:, idx]
            sc = jnp.einsum('bthd,btjhd->bhtj', q_blk[:, :, g], kg).astype(jnp.float32) * scale
            sc = jnp.where(valid[None, None], sc, -jnp.inf)
            m = jnp.max(sc, axis=-1, keepdims=True)
            p = jnp.exp(sc - m)
            l = jnp.sum(p, axis=-1, keepdims=True)
            o = jnp.einsum('bhtj,btjhd->bhtd', p, vg.astype(jnp.float32)) / l
            outs.append(o)
            maxes.append(m)
            sums.append(l)
        m_all = jnp.stack(maxes)
        wts = jnp.stack(sums) * jnp.exp(m_all - jnp.max(m_all, axis=0, keepdims=True))
        out = jnp.sum(wts * jnp.stack(outs), axis=0) / jnp.sum(wts, axis=0)
        return out.transpose(0, 2, 1, 3).astype(q.dtype)

    out = lax.map(block, (q_blocks, starts))
    return out.swapaxes(0, 1).reshape(bsz, s, B_OUT)


def stick_breaking_attention(q, k, v):
    bsz, s = q.shape[0], q.shape[1]
    n_blocks = s // C_QBLOCK
    scale = HEAD_DIM ** -0.5
    q_blocks = q.reshape(bsz, n_blocks, C_QBLOCK, C_HEADS, HEAD_DIM).swapaxes(0, 1)
    starts = jnp.arange(n_blocks) * C_QBLOCK
    kpos = jnp.arange(s)
    v32 = v.astype(jnp.float32)

    def block(args):
        q_blk, start = args
        t = start + jnp.arange(C_QBLOCK)
        z = jnp.einsum('bthd,bshd->bhts', q_blk, k).astype(jnp.float32) * scale
        causal = kpos[None, :] < t[:, None]
        log_one_minus = jnp.where(causal, jax.nn.log_sigmoid(-z), 0.0)
        after = lax.cumsum(log_one_minus, axis=3, reverse=True) - log_one_minus
        a = jnp.where(causal, jnp.exp(jax.nn.log_sigmoid(z) + after), 0.0)
        o = jnp.einsum('bhts,bshd->bthd', a, v32)
        return o.astype(q.dtype)

    out = lax.map(block, (q_blocks, starts))
    return out.swapaxes(0, 1).reshape(bsz, s, C_WIDTH)


def hybrid_mixer(h, w_in, a_ln_g, a_ln_b, a_ws, a_bs, w_pa, w_pb, w_pc, w_o):
    bsz, s = h.shape[0], h.shape[1]
    proj = h @ w_in
    za, qkv_b, qkv_c, gate_pre = jnp.split(proj, [SPLIT_A, SPLIT_B, SPLIT_C], axis=-1)
    y_a = chunked_spatial_gating(jax.nn.gelu(za), a_ln_g, a_ln_b, a_ws, a_bs)
    qb, kb, vb = [t.reshape(bsz, s, B_GROUPS, B_HEADS_PER_GROUP, HEAD_DIM)
                  for t in jnp.split(qkv_b, 3, axis=-1)]
    y_b = dilated_attention(qb, kb, vb)
    qc, kc, vc = [t.reshape(bsz, s, C_HEADS, HEAD_DIM) for t in jnp.split(qkv_c, 3, axis=-1)]
    y_c = stick_breaking_attention(qc, kc, vc)
    g_a, g_b, g_c = jnp.split(jax.nn.sigmoid(gate_pre), 3, axis=-1)
    merged = g_a * (y_a @ w_pa) + g_b * (y_b @ w_pb) + g_c * (y_c @ w_pc)
    return merged @ w_o


def setup_inputs(seed: int = 0) -> dict:
    key = jax.random.key(seed)
    ks = jax.random.split(key, 24)
    f32 = jnp.float32

    def dense(k, fan_in, fan_out):
        return jax.random.normal(k, (DEPTH, fan_in, fan_out), f32) * fan_in ** -0.5

    def gain(k, n):
        return 1.0 + 0.01 * jax.random.normal(k, (DEPTH, n), f32)

    return {
        "x": jax.random.normal(ks[0], (BATCH, SEQ, D_MODEL), f32),
        "ffn1_pre_g": gain(ks[1], D_MODEL),
        "ffn1_wi": dense(ks[2], D_MODEL, 2 * D_FF),
        "ffn1_wo": dense(ks[3], D_FF, D_MODEL),
        "ffn1_post_g": gain(ks[4], D_MODEL),
        "mix_pre_g": gain(ks[5], D_MODEL),
        "w_in": dense(ks[6], D_MODEL, IN_COLS),
        "a_ln_g": gain(ks[7], A_WIDTH),
        "a_ln_b": 0.01 * jax.random.normal(ks[8], (DEPTH, A_WIDTH), f32),
        "a_ws": jax.random.normal(ks[9], (DEPTH, A_GROUPS, A_CHUNK, A_CHUNK), f32) * A_CHUNK ** -0.5,
        "a_bs": 1.0 + 0.01 * jax.random.normal(ks[10], (DEPTH, A_GROUPS, A_CHUNK), f32),
        "w_pa": dense(ks[11], A_WIDTH, D_MODEL),
        "w_pb": dense(ks[12], B_OUT, D_MODEL),
        "w_pc": dense(ks[13], C_WIDTH, D_MODEL),
        "w_o": dense(ks[14], D_MODEL, D_MODEL),
        "mix_post_g": gain(ks[15], D_MODEL),
        "ffn2_pre_g": gain(ks[16], D_MODEL),
        "ffn2_wi": dense(ks[17], D_MODEL, 2 * D_FF),
        "ffn2_wo": dense(ks[18], D_FF, D_MODEL),
        "ffn2_post_g": gain(ks[19], D_MODEL),
    }


def reference(x, ffn1_pre_g, ffn1_wi, ffn1_wo, ffn1_post_g, mix_pre_g, w_in, a_ln_g, a_ln_b,
              a_ws, a_bs, w_pa, w_pb, w_pc, w_o, mix_post_g, ffn2_pre_g, ffn2_wi, ffn2_wo,
              ffn2_post_g):
    for l in range(DEPTH):
        h = rms_norm(x, ffn1_pre_g[l])
        x = x + 0.5 * rms_norm(swiglu(h, ffn1_wi[l], ffn1_wo[l]), ffn1_post_g[l])
        h = rms_norm(x, mix_pre_g[l])
        y = hybrid_mixer(h, w_in[l], a_ln_g[l], a_ln_b[l], a_ws[l], a_bs[l],
                         w_pa[l], w_pb[l], w_pc[l], w_o[l])
        x = x + rms_norm(y, mix_post_g[l])
        h = rms_norm(x, ffn2_pre_g[l])
        x = x + 0.5 * rms_norm(swiglu(h, ffn2_wi[l], ffn2_wo[l]), ffn2_post_g[l])
    return x
```

```python
import numpy as np
import ml_dtypes
from contextlib import ExitStack
import concourse.bass as bass
import concourse.mybir as mybir
from concourse.bass_utils import run_bass_kernel_spmd

F32 = mybir.dt.float32
BF16 = mybir.dt.bfloat16
AF = mybir.ActivationFunctionType
ALU = mybir.AluOpType

ENGINES = ("pe", "act", "dve", "pool", "sp")

D = 1024
S = 2048
T = 512
NT = S // T
DFF = 2816
NFC = DFF // 128
DEPTH = 2
QB0, KB0, VB0 = 1024, 1024 + 768, 1024 + 1536
QC0 = 1024 + 2304
KC0, VC0 = QC0 + 512, QC0 + 1024
GATE0 = QC0 + 1536
EPS = 1e-6
NEG = -30000.0
import os
BLEVEL = int(os.environ.get("BLEVEL", "9"))

G_F1PRE, G_F1POST, G_MPRE, G_MPOST, G_F2PRE, G_F2POST = range(6)


class Buf:
    __slots__ = ("name", "w", "r", "dsem", "dcnt")

    def __init__(self, name=""):
        self.name = name
        self.w = None
        self.r = {}
        self.dsem = None
        self.dcnt = 0

    def inherit(self, *olds):
        for o in olds:
            evs = list(o.r.items())
            if o.w is not None:
                evs.append(o.w)
            for k, v in evs:
                if self.r.get(k, 0) < v:
                    self.r[k] = v
        return self


class Prog:
    def __init__(self, nc, stack):
        self.nc = nc
        self.stack = stack
        self.dry = False
        self.q = {e: [] for e in ENGINES}
        self.sem = {}
        self.cnt = {}
        for e in ENGINES:
            self.sem[e] = stack.enter_context(nc.semaphore("s_" + e))
            self.cnt[e] = 0
        self.seen = {e: {} for e in ENGINES}
        self.ndsem = 0
        self.pending = {e: False for e in ENGINES}
        self.nwait = 0

    def _need(self, eng, dep):
        if dep is None:
            return
        key, val = dep
        if key == "pe" and eng == "pe":
            return
        if self.seen[eng].get(key, 0) >= val:
            return
        if key in self.cnt:
            assert val <= self.cnt[key], ("wait on future signal", eng, key, val, self.cnt[key])
        self.seen[eng][key] = val
        sem = self.sem[key]
        self.nwait += 1
        self.q[eng].append(lambda e, sem=sem, val=val: e.wait_ge(sem, val))

    def _deps(self, eng, reads, writes):
        for b in reads:
            self._need(eng, b.w)
        for b in writes:
            self._need(eng, b.w)
            for k, v in list(b.r.items()):
                self._need(eng, (k, v))

    def op(self, eng, fn, reads=(), writes=(), signal=True):
        if self.dry:
            return
        if eng != "pe":
            excl = [b for b in reads if b.name.startswith("bank") and b not in writes]
            if excl:
                writes = list(writes) + excl
        self._deps(eng, reads, writes)
        val = self.cnt[eng] + 1
        ev = (eng, val)
        for b in reads:
            b.r[eng] = val
        for b in writes:
            b.w = ev
            b.r = {}
        if signal:
            self.cnt[eng] = val
            self.pending[eng] = False
            sem = self.sem[eng]
            self.q[eng].append(lambda e, fn=fn, sem=sem: fn(e).then_inc(sem, 1))
        else:
            assert eng == "pe"
            self.pending[eng] = True
            self.q[eng].append(lambda e, fn=fn: fn(e))

    def dma(self, eng, out, in_, reads=(), writes=(), dbuf=None):
        if self.dry:
            return
        self._deps(eng, reads, writes)
        if dbuf is None:
            dbuf = writes[0] if writes else reads[0]
        if dbuf.dsem is None:
            key = "d%d" % self.ndsem
            self.ndsem += 1
            self.sem[key] = self.stack.enter_context(self.nc.semaphore("s_" + key))
            dbuf.dsem = key
        dbuf.dcnt += 16
        ev = (dbuf.dsem, dbuf.dcnt)
        for b in reads:
            b.r[dbuf.dsem] = dbuf.dcnt
        for b in writes:
            b.w = ev
            b.r = {}
        sem = self.sem[dbuf.dsem]
        self.q[eng].append(lambda e, out=out, in_=in_, sem=sem: e.dma_start(out=out, in_=in_).then_inc(sem, 16))

    def barrier(self):
        if self.dry:
            return
        engs = ("pe", "act", "dve", "pool")
        for e in engs:
            for o in engs:
                if o != e and self.cnt[o] > 0:
                    self._need(e, (o, self.cnt[o]))

    def emit(self):
        nc = self.nc
        assert not any(self.pending.values()), self.pending
        with nc.Block() as block:
            @block.tensor
            def _(e):
                for f in self.q["pe"]:
                    f(e)

            @block.scalar
            def _(e):
                for f in self.q["act"]:
                    f(e)

            @block.vector
            def _(e):
                for f in self.q["dve"]:
                    f(e)

            @block.gpsimd
            def _(e):
                for f in self.q["pool"]:
                    f(e)

            @block.sync
            def _(e):
                for f in self.q["sp"]:
                    f(e)


class SubArena:
    def __init__(self, t, base, cap):
        self.t = t
        self.base = base
        self.cap = cap
        self.off = 0

    def reset(self):
        self.off = 0

    def _alloc(self, nbytes):
        off = (self.off + 63) // 64 * 64
        assert off + nbytes <= self.cap, ("arena overflow", off, nbytes, self.cap)
        self.off = off + nbytes
        return self.base + off

    def bf(self, nelem):
        o = self._alloc(nelem * 2)
        return self.t[:, o // 2: o // 2 + nelem]

    def f32(self, nelem):
        o = self._alloc(nelem * 4)
        return self.t[:, o // 2: o // 2 + 2 * nelem].bitcast(F32)


class WStream:
    def __init__(self, P, classes):
        self.P = P
        self.cls = classes
        self.plan = []
        self.i = 0
        self.next_emit = 0
        self.done = {c: 0 for c in classes}
        self.num = {c: 0 for c in classes}

    def _cls_for(self, nelem):
        best = None
        for c, slots in self.cls.items():
            cap = slots[0][0].shape[1]
            if nelem <= cap and (best is None or cap < self.cls[best][0][0].shape[1]):
                best = c
        assert best is not None, nelem
        return best

    def get(self, src, nk, ncols):
        return self.get_many([(src, nk, ncols)])[0]

    def get_many(self, specs):
        P = self.P
        outs = []
        if P.dry:
            for src, nk, ncols in specs:
                nelem = nk * ncols
                c = self._cls_for(nelem)
                assert len(specs) <= len(self.cls[c])
                self.plan.append((src, nk, ncols, c, self.num[c]))
                self.num[c] += 1
                view, buf = self.cls[c][0]
                outs.append((view[:, 0:nelem].rearrange("p (k n) -> p k n", k=nk), Buf()))
            return outs
        last = self.i + len(specs) - 1
        while self.next_emit < len(self.plan):
            s2, nk2, nc2, c2, k2 = self.plan[self.next_emit]
            n2 = len(self.cls[c2])
            if k2 - self.done[c2] > n2 - 1:
                break
            view2, buf2 = self.cls[c2][k2 % n2]
            dst = view2[:, 0:nk2 * nc2].rearrange("p (k n) -> p k n", k=nk2)
            P.dma("pool", dst, s2, writes=[buf2])
            self.next_emit += 1
        assert self.next_emit > last, ("slab not emitted", self.i)
        for src, nk, ncols in specs:
            src_, nk_, ncols_, c, k = self.plan[self.i]
            assert (nk_, ncols_) == (nk, ncols), ("wstream mismatch", self.i)
            n = len(self.cls[c])
            view, buf = self.cls[c][k % n]
            outs.append((view[:, 0:nk * ncols].rearrange("p (k n) -> p k n", k=nk), buf))
            self.i += 1
        for src, nk, ncols in specs:
            c = self._cls_for(nk * ncols)
            self.done[c] += 1
        return outs


class LazyDram:
    def __init__(self, nc, name, shape, dtype=F32):
        self.nc, self.name, self.shape, self.dtype, self._ap = nc, name, shape, dtype, None

    def __getitem__(self, idx):
        if self._ap is None:
            self._ap = self.nc.dram_tensor(self.name, list(self.shape), self.dtype, kind="ExternalInput").ap()
            DECLARED.append(self.name)
        return self._ap[idx]


DECLARED = []


def build_program(stop=None, plan=None, parts="BCE", dump=None):
    nc = bass.Bass("TRN2", target_bir_lowering=False)
    del DECLARED[:]

    def dram(n, s, d=F32, kind="ExternalInput"):
        if kind == "ExternalInput":
            DECLARED.append(n)
        return nc.dram_tensor(n, list(s), d, kind=kind).ap()
    xin = dram("xT", [D, S])
    out = dram("outT", [D, S], kind="ExternalOutput")
    gains_d = dram("gains", [128, 96])
    cbf_d = dram("cbf", [128, 1536], BF16)
    cf_d = dram("cf32", [128, 132])
    lnrep_d = dram("lnrep", [DEPTH, 128, 1024])
    bsrep_d = dram("bsrep", [DEPTH, 128, 512])
    wsT_d = dram("wsT", [DEPTH, 128, 512])
    ffn_wi = [LazyDram(nc, "ffn1_wi", [DEPTH, D, 2 * DFF]), LazyDram(nc, "ffn2_wi", [DEPTH, D, 2 * DFF])]
    ffn_wo = [LazyDram(nc, "ffn1_wo", [DEPTH, DFF, D]), LazyDram(nc, "ffn2_wo", [DEPTH, DFF, D])]
    w_in = LazyDram(nc, "w_in", [DEPTH, D, 7936])
    w_pa = LazyDram(nc, "w_pa", [DEPTH, 512, D])
    w_pb = LazyDram(nc, "w_pb", [DEPTH, 256, D])
    w_pc = LazyDram(nc, "w_pc", [DEPTH, 512, D])
    w_o = LazyDram(nc, "w_o", [DEPTH, D, D])

    def wview(w2d):
        return w2d.rearrange("(kc p) n -> p kc n", p=128)

    with ExitStack() as st:
        P = Prog(nc, st)
        total = int(nc.sbuf_bytes_remaining) // 64 * 64 - 256
        arena_t = st.enter_context(nc.sbuf_tensor("arena", [128, total // 2], BF16))
        main = SubArena(arena_t, 0, total)
        ps = st.enter_context(nc.psum_tensor("ps", [128, 8, 512], F32))
        pbuf = [Buf("bank%d" % i) for i in range(8)]
        free_banks = list(range(8))

        def bank():
            return free_banks.pop(0)

        def release(b):
            free_banks.append(b)

        xT = main.f32(8 * S).rearrange("p (c t) -> p c t", c=8)
        xbuf = [[Buf("x%d_%d" % (c, ti)) for ti in range(NT)] for c in range(8)]
        hT = main.bf(8 * S).rearrange("p (c t) -> p c t", c=8)
        hbuf = [Buf("h%d" % ti) for ti in range(NT)]
        gains = main.f32(96); bgains = Buf("gains")
        ghalf = main.f32(96); bghalf = Buf("ghalf")
        cbf = main.bf(1536); bcbf = Buf("cbf")
        cf = main.f32(132); bcf = Buf("cf")
        lnrep = main.f32(1024); blnrep = Buf("lnrep")
        bsrep = main.f32(512); bbsrep = Buf("bsrep")
        wsT = main.bf(512); bwsT = Buf("wsT")
        NSMALL, NBIG = 5, 2
        small = [(main.bf(2048), Buf("ws%d" % i)) for i in range(NSMALL)]
        big = [(main.bf(NFC * 128), Buf("wb%d" % i)) for i in range(NBIG)]
        W = WStream(P, {"s": small, "b": big})
        scr_cap = total - ((main.off + 63) // 64 * 64)
        scr = SubArena(arena_t, (main.off + 63) // 64 * 64, scr_cap)
        scr_users = []

        ident = cbf[:, 0:128]
        tri = cbf[:, 128:256]
        ones = cbf[:, 256:384]
        onesL = cbf[:, 384:512]
        onesR = cbf[:, 512:640]
        mrow = cbf[:, 640:1152]
        zmask = cbf[:, 1152:1280]
        amask = cbf[:, 1280:1408]
        maskA = cf[:, 0:128]
        eps_ap = cf[:, 128:129]
        one_ap = cf[:, 129:130]

        def gcol(gi, l, c):
            o = gi * 16 + l * 8 + c
            return gains[:, o:o + 1]

        def ghcol(gi, l, c):
            o = gi * 16 + l * 8 + c
            return ghalf[:, o:o + 1]

        def mm(out_, lhsT, rhs, start, stop, reads, writes, signal=True):
            P.op("pe", lambda e: e.matmul(out_, lhsT, rhs, start=start, stop=stop), reads=reads, writes=writes, signal=signal)

        dump_src = {}

        def new_scratch(bufs):
            for b in bufs:
                b.inherit(*scr_users)
            return bufs

        def body():
            nonlocal scr_users
            free_banks[:] = list(range(8))
            P.dma("sp", gains, gains_d, writes=[bgains])
            P.dma("sp", cbf, cbf_d, writes=[bcbf])
            P.dma("sp", cf, cf_d, writes=[bcf])
            for ti in range(NT):
                P.dma("sp", xT[:, :, ti * T:(ti + 1) * T], xin.rearrange("(c p) t -> p c t", p=128)[:, :, ti * T:(ti + 1) * T],
                      writes=[xbuf[c][ti] for c in range(8)], dbuf=xbuf[0][ti])
            P.op("dve", lambda e: e.tensor_scalar(ghalf, gains, 0.5, None, ALU.mult), reads=[bgains], writes=[bghalf])

            def rstd_from_bank(bss, dst, dstbuf):
                P.op("act", lambda e: e.activation(dst, ps[:, bss, :], AF.Sqrt, bias=eps_ap, scale=1.0 / D),
                     reads=[pbuf[bss], bcf], writes=[dstbuf])
                P.op("dve", lambda e: e.reciprocal(dst, dst), reads=[dstbuf], writes=[dstbuf])

            def prenorm(l, gi, ti, sq, bsq, rstd, brstd):
                tok = slice(ti * T, (ti + 1) * T)
                bss = bank()
                for c in range(8):
                    k = c % 2
                    P.op("act", lambda e, c=c, k=k: e.activation(sq[k], xT[:, c, tok], AF.Square),
                         reads=[xbuf[c][ti]], writes=[bsq[k]])
                    mm(ps[:, bss, :], ones, sq[k], c == 0, c == 7, [bsq[k], bcbf], [pbuf[bss]])
                rstd_from_bank(bss, rstd, brstd)
                release(bss)
                for c in range(8):
                    P.op("dve", lambda e, c=c: e.scalar_tensor_tensor(hT[:, c, tok], xT[:, c, tok], gcol(gi, l, c), rstd, ALU.mult, ALU.mult),
                         reads=[xbuf[c][ti], brstd, bgains], writes=[hbuf[ti]])

            def prenorm_a(ti, sq8, bsq8):
                tok = slice(ti * T, (ti + 1) * T)
                for c in range(8):
                    P.op("act", lambda e, c=c: e.activation(sq8[c], xT[:, c, tok], AF.Square),
                         reads=[xbuf[c][ti]], writes=[bsq8[c]])

            def prenorm_b(l, gi, ti, sq8, bsq8, rstd, brstd):
                tok = slice(ti * T, (ti + 1) * T)
                bss = bank()
                for c in range(8):
                    mm(ps[:, bss, :], ones, sq8[c], c == 0, c == 7, [bsq8[c], bcbf], [pbuf[bss]], signal=(c == 7))
                rstd_from_bank(bss, rstd, brstd)
                release(bss)
                for c in range(8):
                    P.op("dve", lambda e, c=c: e.scalar_tensor_tensor(hT[:, c, tok], xT[:, c, tok], gcol(gi, l, c), rstd, ALU.mult, ALU.mult),
                         reads=[xbuf[c][ti], brstd, bgains], writes=[hbuf[ti]])

            def postnorm_residual(l, gi, ti, ybuf, bybuf, bss, rstd, brstd, tmp, btmp, half):
                tok = slice(ti * T, (ti + 1) * T)
                rstd_from_bank(bss, rstd, brstd)
                release(bss)
                for c in range(8):
                    k = c % 2
                    sc = ghcol(gi, l, c) if half else gcol(gi, l, c)
                    P.op("dve", lambda e, c=c, k=k, sc=sc: e.scalar_tensor_tensor(tmp[k], ybuf[:, c, :], sc, rstd, ALU.mult, ALU.mult),
                         reads=[bybuf[c], brstd, bghalf, bgains], writes=[btmp[k]])
                    P.op("pool", lambda e, c=c, k=k: e.tensor_tensor(xT[:, c, tok], xT[:, c, tok], tmp[k], ALU.add),
                         reads=[btmp[k], xbuf[c][ti]], writes=[xbuf[c][ti]])

            def ffn(l, which):
                nonlocal scr_users
                scr.reset()
                gi_pre = G_F1PRE if which == 0 else G_F2PRE
                gi_post = G_F1POST if which == 0 else G_F2POST
                wi = wview(ffn_wi[which][l])
                wo = wview(ffn_wo[which][l])
                act = scr.bf(NFC * T).rearrange("p (c t) -> p c t", c=NFC)
                bact = [Buf("act%d" % j) for j in range(NFC)]
                ybuf = scr.f32(8 * T).rearrange("p (c t) -> p c t", c=8)
                bybuf = [Buf("y%d" % c) for c in range(8)]
                sq = [scr.bf(T) for _ in range(2)]; bsq = [Buf("sq%d" % i) for i in range(2)]
                stmp = [scr.f32(T) for _ in range(2)]; bstmp = [Buf("st%d" % i) for i in range(2)]
                rstd = scr.f32(T); brstd = Buf("rstd")
                tmp = [scr.f32(T) for _ in range(2)]; btmp = [Buf("tmp%d" % i) for i in range(2)]
                sq8 = [scr.bf(T) for _ in range(8)]; bsq8 = [Buf("sq8_%d" % i) for i in range(8)]
                rstdp = scr.f32(T); brstdp = Buf("rstdp")
                mine = bact + bybuf + bsq + bstmp + [brstd] + btmp + bsq8 + [brstdp]
                new_scratch(mine)
                prenorm_a(0, sq8, bsq8)
                prenorm_b(l, gi_pre, 0, sq8, bsq8, rstdp, brstdp)
                for ti in range(NT):
                    tok = slice(ti * T, (ti + 1) * T)
                    for jp in range(NFC // 2):
                        if ti + 1 < NT and jp == 4:
                            prenorm_a(ti + 1, sq8, bsq8)
                        if ti + 1 < NT and jp == 8:
                            prenorm_b(l, gi_pre, ti + 1, sq8, bsq8, rstdp, brstdp)
                        (gs, bgs), (us, bus) = W.get_many([(wi[:, :, jp * 256:(jp + 1) * 256], 8, 256),
                                                           (wi[:, :, DFF + jp * 256:DFF + (jp + 1) * 256], 8, 256)])
                        for jj in range(2):
                            j = jp * 2 + jj
                            bg = bank(); bu = bank()
                            for kc in range(8):
                                mm(ps[:, bg, :], gs[:, kc, jj * 128:(jj + 1) * 128], hT[:, kc, tok], kc == 0, kc == 7,
                                   [bgs, hbuf[ti]], [pbuf[bg]], signal=(kc == 7))
                            for kc in range(8):
                                mm(ps[:, bu, :], us[:, kc, jj * 128:(jj + 1) * 128], hT[:, kc, tok], kc == 0, kc == 7,
                                   [bus, hbuf[ti]], [pbuf[bu]], signal=(kc == 7))
                            k = j % 2
                            P.op("act", lambda e, k=k, bg=bg: e.activation(stmp[k], ps[:, bg, :], AF.Silu),
                                 reads=[pbuf[bg]], writes=[bstmp[k]])
                            P.op("dve", lambda e, k=k, bu=bu, j=j: e.tensor_tensor(act[:, j, :], stmp[k], ps[:, bu, :], ALU.mult),
                                 reads=[bstmp[k], pbuf[bu]], writes=[bact[j]])
                            release(bg); release(bu)
                    bss = bank()
                    for m in range(8):
                        ws_, bws = W.get(wo[:, :, m * 128:(m + 1) * 128], NFC, 128)
                        b = bank()
                        for fc in range(NFC):
                            mm(ps[:, b, :], ws_[:, fc, :], act[:, fc, :], fc == 0, fc == NFC - 1,
                               [bws, bact[fc]], [pbuf[b]], signal=(fc == NFC - 1))
                        P.op("act", lambda e, m=m, b=b: e.activation(ybuf[:, m, :], ps[:, b, :], AF.Copy),
                             reads=[pbuf[b]], writes=[bybuf[m]])
                        k = m % 2
                        P.op("act", lambda e, k=k, b=b: e.activation(sq[k], ps[:, b, :], AF.Square),
                             reads=[pbuf[b]], writes=[bsq[k]])
                        release(b)
                        if m > 0:
                            mm(ps[:, bss, :], ones, sq[(m - 1) % 2], m == 1, False, [bsq[(m - 1) % 2], bcbf], [pbuf[bss]])
                    mm(ps[:, bss, :], ones, sq[7 % 2], False, True, [bsq[7 % 2], bcbf], [pbuf[bss]])
                    postnorm_residual(l, gi_post, ti, ybuf, bybuf, bss, rstd, brstd, tmp, btmp, True)
                scr_users = mine

            def mixer(l):
                nonlocal scr_users
                scr.reset()
                win = wview(w_in[l])
                P.dma("sp", lnrep, lnrep_d[l], writes=[blnrep])
                P.dma("sp", bsrep, bsrep_d[l], writes=[bbsrep])
                yb = scr.bf(2 * S).rearrange("p (c t) -> p c t", c=2)
                byb = [Buf("yb%d" % j) for j in range(2)]
                yc = scr.bf(4 * S).rearrange("p (c t) -> p c t", c=4)
                byc = [[Buf("yc%d_%d" % (j, ti)) for ti in range(NT)] for j in range(4)]
                keep = byb + [b for row in byc for b in row]
                new_scratch(keep)
                dump_src["yb"] = (yb, byb, 2)
                dump_src["yc"] = (yc, [b for row in byc for b in row], 4)
                scr_mark = scr.off
                sq = [scr.bf(T) for _ in range(2)]; bsq = [Buf() for _ in range(2)]
                rstd = scr.f32(T); brstd = Buf()
                wtmp = scr.f32(512); bwtmp = Buf()
                pre = new_scratch(bsq + [brstd, bwtmp])
                P.dma("sp", wtmp, wsT_d[l], writes=[bwtmp])
                for g in range(4):
                    P.op("dve", lambda e, g=g: e.tensor_tensor(wsT[:, g * 128:(g + 1) * 128], wtmp[:, g * 128:(g + 1) * 128], maskA, ALU.mult),
                         reads=[bwtmp, bcf], writes=[bwsT])
                for ti in range(NT):
                    prenorm(l, G_MPRE, ti, sq, bsq, rstd, brstd)
                scr_users = keep + pre

                scr.off = scr_mark
                P.barrier()
                if "B" in parts:
                    qpz = [scr.bf(S) for _ in range(2)]; kperm = scr.bf(S)
                    bqp = Buf("qperm"); bkp = Buf("kperm")
                    vz = scr.bf(16 * 2 * 128).rearrange("p (b h d) -> p b h d", b=16, h=2)
                    bvz = [Buf("vz%d" % i) for i in range(4)]
                    numacc = scr.f32(S); denacc = scr.f32(S)
                    bnum = Buf("num"); bden = Buf("den")
                    pT = [scr.bf(512) for _ in range(2)]; bpT = [Buf() for _ in range(2)]
                    mineB = new_scratch([bqp, bkp] + bvz + [bnum, bden] + bpT)
                    for i4 in range(4):
                        P.op("pool", lambda e, i4=i4: e.memset(vz[:, i4 * 4:(i4 + 1) * 4], 0.0), writes=[bvz[i4]])
                    for hh in range(2):
                        P.op("pool", lambda e, hh=hh: e.memset(qpz[hh], 0.0), writes=[bqp])
                    pcount = 0
                    for j in range(2):
                        for g in range(3):
                            r = (1, 4, 16)[g]
                            Lc = S // r
                            nblk = Lc // 128
                            (sq_, bsq_), (sk_, bsk_), (sv_, bsv_) = W.get_many([
                                (win[:, :, QB0 + g * 256 + j * 128: QB0 + g * 256 + (j + 1) * 128], 8, 128),
                                (win[:, :, KB0 + g * 256 + j * 128: KB0 + g * 256 + (j + 1) * 128], 8, 128),
                                (win[:, :, VB0 + g * 256 + j * 128: VB0 + g * 256 + (j + 1) * 128], 8, 128)])
                            qp3 = [qpz[hh].rearrange("p (c i) -> p c i", c=r) for hh in range(2)]
                            kp3 = kperm.rearrange("p (c i) -> p c i", c=r)
                            w = T // r
                            for ti in range(NT):
                                tok = slice(ti * T, (ti + 1) * T)
                                bq = bank(); bk = bank()
                                for kc in range(8):
                                    mm(ps[:, bq, :], sq_[:, kc, :], hT[:, kc, tok], kc == 0, kc == 7, [bsq_, hbuf[ti]], [pbuf[bq]], signal=(kc == 7))
                                for kc in range(8):
                                    mm(ps[:, bk, :], sk_[:, kc, :], hT[:, kc, tok], kc == 0, kc == 7, [bsk_, hbuf[ti]], [pbuf[bk]], signal=(kc == 7))
                                for hh in range(2):
                                    hs = slice(hh * 64, (hh + 1) * 64)
                                    P.op("act", lambda e, bq=bq, ti=ti, qp3=qp3, r=r, w=w, hh=hh, hs=hs: e.activation(
                                        qp3[hh][hs, :, ti * w:(ti + 1) * w], ps[hs, bq, :].rearrange("p (i c) -> p c i", c=r), AF.Copy, scale=0.125),
                                        reads=[pbuf[bq]], writes=[bqp])
                                P.op("dve", lambda e, bk=bk, ti=ti, kp3=kp3, r=r, w=w: e.tensor_copy(
                                    kp3[:, :, ti * w:(ti + 1) * w], ps[:, bk, :].rearrange("p (i c) -> p c i", c=r)),
                                    reads=[pbuf[bk]], writes=[bkp])
                                release(bq); release(bk)
                            for b4 in range(4):
                                bv = bank()
                                for bb in range(4):
                                    bp = b4 * 4 + bb
                                    p0 = bp * 128
                                    c = p0 // Lc
                                    i0 = p0 % Lc
                                    t0 = c + r * i0
                                    for kc in range(8):
                                        mm(ps[:, bv, bb * 128:(bb + 1) * 128], hT[:, kc, t0:t0 + r * 127 + 1:r], sv_[:, kc, :],
                                           kc == 0, kc == 7, [bsv_] + hbuf, [pbuf[bv]], signal=(kc == 7))
                                pv = ps[:, bv, :].rearrange("p (b d) -> p b d", b=4)
                                P.op("dve", lambda e, b4=b4, pv=pv: e.tensor_copy(vz[:, b4 * 4:(b4 + 1) * 4, 0, 0:64], pv[:, :, 0:64]),
                                     reads=[pbuf[bv]], writes=[bvz[b4]])
                                P.op("act", lambda e, b4=b4, pv=pv: e.activation(vz[:, b4 * 4:(b4 + 1) * 4, 1, 64:128], pv[:, :, 64:128], AF.Copy),
                                     reads=[pbuf[bv]], writes=[bvz[b4]])
                                release(bv)
                            for c in range(r if BLEVEL >= 2 else 0):
                                for qb in range(nblk):
                                    bpq = c * nblk + qb
                                    kbs = ([bpq - 1] if qb > 0 else []) + [bpq]
                                    nk = len(kbs)
                                    Wd = nk * 256
                                    bs_ = bank()
                                    mm(ps[:, bs_, 0:Wd], ident, mrow[:, 512 - Wd:512], True, False, [bcbf], [pbuf[bs_]], signal=False)
                                    for a, kb in enumerate(kbs):
                                        for hh in range(2):
                                            last = (a == nk - 1 and hh == 1)
                                            mm(ps[:, bs_, a * 256 + hh * 128: a * 256 + (hh + 1) * 128],
                                               kperm[:, kb * 128:(kb + 1) * 128],
                                               qpz[hh][:, bpq * 128:(bpq + 1) * 128],
                                               False, last, [bqp, bkp], [pbuf[bs_]], signal=last)
                                    k = pcount % 2
                                    pcount += 1
                                    P.op("act", lambda e, k=k, bs_=bs_, Wd=Wd: e.activation(pT[k][:, 0:Wd], ps[:, bs_, 0:Wd], AF.Exp),
                                         reads=[pbuf[bs_]], writes=[bpT[k]])
                                    release(bs_)
                                    if BLEVEL < 3:
                                        continue
                                    bo = bank()
                                    n = 0
                                    for a, kb in enumerate(kbs):
                                        for hh in range(2):
                                            n += 1
                                            mm(ps[:, bo, 0:128], vz[:, kb, hh, :], pT[k][:, a * 256 + hh * 128: a * 256 + (hh + 1) * 128],
                                               n == 1, n == 2 * nk, [bvz[kb // 4], bpT[k]], [pbuf[bo]], signal=(n == 2 * nk))
                                    n = 0
                                    for a, kb in enumerate(kbs):
                                        for hh in range(2):
                                            n += 1
                                            mm(ps[:, bo, 128:256], onesL if hh == 0 else onesR, pT[k][:, a * 256 + hh * 128: a * 256 + (hh + 1) * 128],
                                               n == 1, n == 2 * nk, [bcbf, bpT[k]], [pbuf[bo]], signal=(n == 2 * nk))
                                    cols = slice(c + r * qb * 128, c + r * qb * 128 + r * 127 + 1, r)
                                    if BLEVEL < 4:
                                        P.op("dve", lambda e, bo=bo: e.tensor_copy(numacc[:, 0:256], ps[:, bo, 0:256]), reads=[pbuf[bo]], writes=[bnum])
                                        release(bo)
                                        continue
                                    if g == 0:
                                        P.op("dve", lambda e, bo=bo, cols=cols: e.tensor_copy(numacc[:, cols], ps[:, bo, 0:128]),
                                             reads=[pbuf[bo]], writes=[bnum])
                                        P.op("act", lambda e, bo=bo, cols=cols: e.activation(denacc[:, cols], ps[:, bo, 128:256], AF.Copy),
                                             reads=[pbuf[bo]], writes=[bden])
                                    else:
                                        P.op("dve", lambda e, bo=bo, cols=cols: e.tensor_tensor(numacc[:, cols], numacc[:, cols], ps[:, bo, 0:128], ALU.add),
                                             reads=[pbuf[bo]], writes=[bnum])
                                        P.op("dve", lambda e, bo=bo, cols=cols: e.tensor_tensor(denacc[:, cols], denacc[:, cols], ps[:, bo, 128:256], ALU.add),
                                             reads=[pbuf[bo]], writes=[bden])
                                    release(bo)
                        if BLEVEL < 4:
                            P.op("dve", lambda e: e.memset(denacc, 1.0), writes=[bden])
                            P.op("dve", lambda e: e.memset(numacc, 1.0), writes=[bnum])
                        P.op("dve", lambda e: e.reciprocal(denacc, denacc), reads=[bden], writes=[bden])
                        P.op("dve", lambda e, j=j: e.tensor_tensor(yb[:, j, :], numacc, denacc, ALU.mult), reads=[bnum, bden], writes=[byb[j]])
                    scr_users = keep + mineB

                scr.off = scr_mark
                P.barrier()
                if "C" in parts:
                    qz = [scr.bf(S) for _ in range(2)]; kT = scr.bf(S); nkT = scr.bf(S)
                    bqT = Buf("qT"); bkT = Buf("kT"); bnkT = Buf("nkT")
                    vzc = scr.bf(16 * 2 * 128).rearrange("p (b h d) -> p b h d", b=16, h=2)
                    bvzc = [Buf("cvz%d" % i) for i in range(4)]
                    NR = 3
                    ebuf = [scr.f32(T) for _ in range(NR)]; beb = [Buf() for _ in range(NR)]
                    Lbuf = [scr.bf(T) for _ in range(NR)]; bLb = [Buf() for _ in range(NR)]
                    Abuf = [scr.bf(T) for _ in range(NR)]; bAb = [Buf() for _ in range(NR)]
                    Rb = [scr.bf(T) for _ in range(2)]; bRb = [Buf() for _ in range(2)]
                    mineC = new_scratch([bqT, bkT, bnkT] + bvzc + beb + bLb + bAb + bRb)
                    for i4 in range(4):
                        P.op("pool", lambda e, i4=i4: e.memset(vzc[:, i4 * 4:(i4 + 1) * 4], 0.0), writes=[bvzc[i4]])
                    for hh in range(2):
                        P.op("pool", lambda e, hh=hh: e.memset(qz[hh], 0.0), writes=[bqT])
                    itc = 0
                    for j in range(4):
                        (sq_, bsq_), (sk_, bsk_), (sv_, bsv_) = W.get_many([
                            (win[:, :, QC0 + j * 128: QC0 + (j + 1) * 128], 8, 128),
                            (win[:, :, KC0 + j * 128: KC0 + (j + 1) * 128], 8, 128),
                            (win[:, :, VC0 + j * 128: VC0 + (j + 1) * 128], 8, 128)])
                        for ti in range(NT):
                            tok = slice(ti * T, (ti + 1) * T)
                            bq = bank(); bk = bank()
                            for kc in range(8):
                                mm(ps[:, bq, :], sq_[:, kc, :], hT[:, kc, tok], kc == 0, kc == 7, [bsq_, hbuf[ti]], [pbuf[bq]], signal=(kc == 7))
                            for kc in range(8):
                                mm(ps[:, bk, :], sk_[:, kc, :], hT[:, kc, tok], kc == 0, kc == 7, [bsk_, hbuf[ti]], [pbuf[bk]], signal=(kc == 7))
                            for hh in range(2):
                                hs = slice(hh * 64, (hh + 1) * 64)
                                P.op("act", lambda e, bq=bq, tok=tok, hh=hh, hs=hs: e.activation(qz[hh][hs, tok], ps[hs, bq, :], AF.Copy, scale=0.125),
                                     reads=[pbuf[bq]], writes=[bqT])
                            P.op("dve", lambda e, bk=bk, tok=tok: e.tensor_copy(kT[:, tok], ps[:, bk, :]), reads=[pbuf[bk]], writes=[bkT])
                            P.op("act", lambda e, bk=bk, tok=tok: e.activation(nkT[:, tok], ps[:, bk, :], AF.Copy, scale=-1.0),
                                 reads=[pbuf[bk]], writes=[bnkT])
                            release(bq); release(bk)
                        for b4 in range(4):
                            bv = bank()
                            for bb in range(4):
                                blk = b4 * 4 + bb
                                for kc in range(8):
                                    mm(ps[:, bv, bb * 128:(bb + 1) * 128], hT[:, kc, blk * 128:(blk + 1) * 128], sv_[:, kc, :],
                                       kc == 0, kc == 7, [bsv_, hbuf[blk // 4]], [pbuf[bv]], signal=(kc == 7))
                            pv = ps[:, bv, :].rearrange("p (b d) -> p b d", b=4)
                            P.op("dve", lambda e, b4=b4, pv=pv: e.tensor_copy(vzc[:, b4 * 4:(b4 + 1) * 4, 0, 0:64], pv[:, :, 0:64]),
                                 reads=[pbuf[bv]], writes=[bvzc[b4]])
                            P.op("act", lambda e, b4=b4, pv=pv: e.activation(vzc[:, b4 * 4:(b4 + 1) * 4, 1, 64:128], pv[:, :, 64:128], AF.Copy),
                                 reads=[pbuf[bv]], writes=[bvzc[b4]])
                            release(bv)
                        for ti in range(NT):
                            bo = bank()
                            nsteps = 4 * ti + 4
                            items = [(hh, step) for step in range(nsteps) for hh in range(2)]
                            nit = len(items)
                            state = {}
                            for hh in range(2):
                                P.op("pool", lambda e, hh=hh: e.memset(Rb[hh], 0.0), writes=[bRb[hh]])

                            def info(n):
                                hh, step = items[n]
                                sb = 4 * ti + 3 - step
                                a = sb - 4 * ti
                                col0 = 128 * a if a > 0 else 0
                                return hh, step, sb, a >= 0, col0

                            zstate = {}

                            def S1a(n):
                                hh, step, sb, diag, col0 = info(n)
                                bz = bank()
                                zstate[n] = bz
                                mm(ps[:, bz, col0:T], kT[:, sb * 128:(sb + 1) * 128], qz[hh][:, ti * T + col0:(ti + 1) * T],
                                   True, not diag, [bkT, bqT], [pbuf[bz]], signal=True)
                                if diag:
                                    mm(ps[:, bz, col0:col0 + 128], ident, zmask, False, True, [bcbf], [pbuf[bz]], signal=True)

                            def S1b(n):
                                hh, step, sb, diag, col0 = info(n)
                                k = (itc + n) % NR
                                bz = zstate.pop(n)
                                P.op("act", lambda e: e.activation(ebuf[k][:, col0:T], ps[:, bz, col0:T], AF.Exp),
                                     reads=[pbuf[bz]], writes=[beb[k]])
                                release(bz)

                            def S2(n):
                                hh, step, sb, diag, col0 = info(n)
                                k = (itc + n) % NR
                                hs = slice(hh * 64, (hh + 1) * 64)
                                P.op("act", lambda e: e.activation(Lbuf[k][:, col0:T], ebuf[k][:, col0:T], AF.Ln, bias=one_ap),
                                     reads=[beb[k], bcf], writes=[bLb[k]])
                                ba = bank()
                                state[n] = ba
                                mm(ps[:, ba, col0:T], tri, Lbuf[k][:, col0:T], True, False, [bcbf, bLb[k]], [pbuf[ba]], signal=False)
                                if step > 0:
                                    mm(ps[:, ba, col0:T], ones, Rb[hh][:, col0:T], False, False, [bcbf, bRb[hh]], [pbuf[ba]], signal=False)
                                mm(ps[:, ba, col0:T], nkT[:, sb * 128:(sb + 1) * 128], qz[hh][:, ti * T + col0:(ti + 1) * T],
                                   False, not diag, [bnkT, bqT], [pbuf[ba]], signal=True)
                                if diag:
                                    mm(ps[:, ba, col0:col0 + 128], ident, amask, False, True, [bcbf], [pbuf[ba]], signal=True)
                                if step < nsteps - 1:
                                    P.op("pool", lambda e: e.tensor_tensor(Rb[hh][:, col0:T], Rb[hh][:, col0:T], Lbuf[k][:, col0:T], ALU.add),
                                         reads=[bLb[k]], writes=[bRb[hh]])

                            def S3(n):
                                hh, step, sb, diag, col0 = info(n)
                                k = (itc + n) % NR
                                ba = state.pop(n)
                                P.op("act", lambda e: e.activation(Abuf[k][:, col0:T], ps[:, ba, col0:T], AF.Exp, scale=-1.0),
                                     reads=[pbuf[ba]], writes=[bAb[k]])
                                release(ba)
                                mm(ps[:, bo, col0:T], vzc[:, sb, hh, :], Abuf[k][:, col0:T], n == 0, n == nit - 1,
                                   [bvzc[sb // 4], bAb[k]], [pbuf[bo]], signal=True)

                            for n in range(nit + 2):
                                if n < nit:
                                    S1a(n)
                                if 0 <= n - 1 < nit:
                                    S2(n - 1)
                                if 0 <= n - 2 < nit:
                                    S3(n - 2)
                                if n < nit:
                                    S1b(n)
                            itc += nit
                            tok = slice(ti * T, (ti + 1) * T)
                            P.op("dve", lambda e, bo=bo, j=j, tok=tok: e.tensor_copy(yc[:, j, tok], ps[:, bo, :]), reads=[pbuf[bo]], writes=[byc[j][ti]])
                            release(bo)
                    scr_users = keep + mineC

                scr.off = scr_mark
                P.barrier()
                if "E" in parts:
                    merged = scr.bf(8 * T).rearrange("p (c t) -> p c t", c=8); bmer = [Buf() for _ in range(8)]
                    obuf = scr.f32(8 * T).rearrange("p (c t) -> p c t", c=8); bob = [Buf() for _ in range(8)]
                    uT = scr.bf(4 * T).rearrange("p (c t) -> p c t", c=4); buT = [Buf() for _ in range(4)]
                    yaT = scr.bf(4 * T).rearrange("p (c t) -> p c t", c=4); byaT = Buf()
                    vtmp = scr.f32(T); bvt = Buf()
                    vln = scr.bf(T); bvln = Buf()
                    svt = vtmp; bsvt = bvt
                    gsb = [scr.f32(T) for _ in range(3)]; bgsb = [Buf() for _ in range(3)]
                    sqe = [scr.bf(T) for _ in range(2)]; bsqe = [Buf() for _ in range(2)]
                    rstd = gsb[2]; brstd = bgsb[2]
                    tmp = gsb[0:2]; btmp = bgsb[0:2]
                    st6 = scr.f32(8); bst6 = Buf()
                    mv = scr.f32(4); bmv = Buf()
                    mineE = new_scratch(bmer + bob + buT + [byaT, bvt, bvln] + bgsb + bsqe + [bst6, bmv])
                    lng = lnrep[:, 0:512]
                    lnb = lnrep[:, 512:1024]
                    wpa = wview(w_pa[l]); wpb = wview(w_pb[l]); wpc = wview(w_pc[l]); wo_ = wview(w_o[l])
                    for ti in range(NT):
                        tok = slice(ti * T, (ti + 1) * T)
                        for up in range(2):
                            us, bus = W.get(win[:, :, up * 256:(up + 1) * 256], 8, 256)
                            for uu in range(2):
                                c = up * 2 + uu
                                b = bank()
                                for kc in range(8):
                                    mm(ps[:, b, :], us[:, kc, uu * 128:(uu + 1) * 128], hT[:, kc, tok], kc == 0, kc == 7,
                                       [bus, hbuf[ti]], [pbuf[b]], signal=(kc == 7))
                                P.op("act", lambda e, c=c, b=b: e.activation(uT[:, c, :], ps[:, b, :], AF.Gelu_apprx_tanh),
                                     reads=[pbuf[b]], writes=[buT[c]])
                                release(b)
                        vs = W.get_many([(win[:, :, 512 + hf * 256: 512 + (hf + 1) * 256], 8, 256) for hf in range(2)])
                        for blk in range(4):
                            tb = slice(ti * T + blk * 128, ti * T + (blk + 1) * 128)
                            b = bank()
                            for hf in range(2):
                                for kc in range(8):
                                    mm(ps[:, b, hf * 256:(hf + 1) * 256], hT[:, kc, tb], vs[hf][0][:, kc, :], kc == 0, kc == 7,
                                       [vs[hf][1], hbuf[ti]], [pbuf[b]], signal=(kc == 7))
                            P.op("act", lambda e, b=b: e.activation(vtmp, ps[:, b, :], AF.Gelu_apprx_tanh), reads=[pbuf[b]], writes=[bvt])
                            release(b)
                            P.op("dve", lambda e: e.bn_stats(st6[:, 0:6], vtmp), reads=[bvt], writes=[bst6])
                            P.op("dve", lambda e: e.bn_aggr(mv[:, 0:2], st6[:, 0:6]), reads=[bst6], writes=[bmv])
                            P.op("act", lambda e: e.activation(mv[:, 2:3], mv[:, 1:2], AF.Sqrt, bias=eps_ap), reads=[bmv, bcf], writes=[bmv])
                            P.op("dve", lambda e: e.reciprocal(mv[:, 2:3], mv[:, 2:3]), reads=[bmv], writes=[bmv])
                            P.op("dve", lambda e: e.tensor_scalar(vtmp, vtmp, mv[:, 0:1], mv[:, 2:3], ALU.subtract, ALU.mult),
                                 reads=[bmv, bvt], writes=[bvt])
                            P.op("pool", lambda e: e.tensor_tensor(vtmp, vtmp, lng, ALU.mult), reads=[bvt, blnrep], writes=[bvt])
                            P.op("pool", lambda e: e.tensor_tensor(vln, vtmp, lnb, ALU.add), reads=[bvt, blnrep], writes=[bvln])
                            b = bank()
                            for g in range(4):
                                mm(ps[:, b, g * 128:(g + 1) * 128], vln[:, g * 128:(g + 1) * 128], wsT[:, g * 128:(g + 1) * 128], True, True,
                                   [bvln, bwsT], [pbuf[b]], signal=(g == 3))
                            P.op("dve", lambda e, b=b: e.tensor_tensor(svt, ps[:, b, :], bsrep, ALU.add), reads=[pbuf[b], bbsrep], writes=[bsvt])
                            release(b)
                            P.op("dve", lambda e, blk=blk: e.tensor_tensor(yaT[:, :, blk * 128:(blk + 1) * 128],
                                                                            svt.rearrange("p (g t) -> p g t", g=4),
                                                                            uT[:, :, blk * 128:(blk + 1) * 128], ALU.mult),
                                 reads=[bsvt] + buT, writes=[byaT])
                        for m in range(8):
                            mc = slice(m * 128, (m + 1) * 128)
                            gb_ = []
                            for br in range(3):
                                gsl, bgsl = W.get(win[:, :, GATE0 + br * 1024 + m * 128: GATE0 + br * 1024 + (m + 1) * 128], 8, 128)
                                b = bank()
                                for kc in range(8):
                                    mm(ps[:, b, :], gsl[:, kc, :], hT[:, kc, tok], kc == 0, kc == 7, [bgsl, hbuf[ti]], [pbuf[b]], signal=(kc == 7))
                                P.op("act", lambda e, br=br, b=b: e.activation(gsb[br], ps[:, b, :], AF.Sigmoid), reads=[pbuf[b]], writes=[bgsb[br]])
                                release(b)
                            srcs = [(wpa, 4, yaT, lambda kc: [byaT]), (wpb, 2, yb, lambda kc: [byb[kc]]), (wpc, 4, yc, lambda kc: [byc[kc][ti]])]
                            for br, (wv, nk, ysrc, bf_) in enumerate(srcs):
                                psl, bpsl = W.get(wv[:, :, mc], nk, 128)
                                b = bank()
                                for kc in range(nk):
                                    rhs = ysrc[:, kc, :] if br == 0 else ysrc[:, kc, tok]
                                    mm(ps[:, b, :], psl[:, kc, :], rhs, kc == 0, kc == nk - 1, [bpsl] + bf_(kc), [pbuf[b]], signal=(kc == nk - 1))
                                P.op("dve", lambda e, br=br, b=b: e.tensor_tensor(gsb[br], gsb[br], ps[:, b, :], ALU.mult),
                                     reads=[pbuf[b], bgsb[br]], writes=[bgsb[br]])
                                release(b)
                            P.op("pool", lambda e: e.tensor_tensor(gsb[0], gsb[0], gsb[1], ALU.add), reads=[bgsb[0], bgsb[1]], writes=[bgsb[0]])
                            P.op("pool", lambda e, m=m: e.tensor_tensor(merged[:, m, :], gsb[0], gsb[2], ALU.add), reads=[bgsb[0], bgsb[2]], writes=[bmer[m]])
                        bss = bank()
                        for m in range(8):
                            osl, bosl = W.get(wo_[:, :, m * 128:(m + 1) * 128], 8, 128)
                            b = bank()
                            for kc in range(8):
                                mm(ps[:, b, :], osl[:, kc, :], merged[:, kc, :], kc == 0, kc == 7, [bosl, bmer[kc]], [pbuf[b]], signal=(kc == 7))
                            P.op("act", lambda e, m=m, b=b: e.activation(obuf[:, m, :], ps[:, b, :], AF.Copy), reads=[pbuf[b]], writes=[bob[m]])
                            k = m % 2
                            P.op("act", lambda e, k=k, b=b: e.activation(sqe[k], ps[:, b, :], AF.Square), reads=[pbuf[b]], writes=[bsqe[k]])
                            release(b)
                            if m > 0:
                                mm(ps[:, bss, :], ones, sqe[(m - 1) % 2], m == 1, False, [bsqe[(m - 1) % 2], bcbf], [pbuf[bss]])
                        mm(ps[:, bss, :], ones, sqe[7 % 2], False, True, [bsqe[7 % 2], bcbf], [pbuf[bss]])
                        postnorm_residual(l, G_MPOST, ti, obuf, bob, bss, rstd, brstd, tmp, btmp, False)
                    scr_users = keep + mineE

            plan_ = plan
            if plan_ is None:
                plan_ = []
                for l in range(DEPTH):
                    plan_ += [("ffn1", l), ("mix", l), ("ffn2", l)]
            for name, l in plan_:
                if name == "ffn1":
                    ffn(l, 0)
                elif name == "ffn2":
                    ffn(l, 1)
                else:
                    mixer(l)
                if stop == "%s_%d" % (name, l):
                    break
            outv = out.rearrange("(c p) t -> p c t", p=128)
            if dump is not None:
                src, bufs, nch = dump_src[dump]
                P.dma("pool", outv[:, 0:nch, :], src, reads=bufs, dbuf=bufs[0])
                if not P.dry:
                    P._need("sp", (bufs[0].dsem, bufs[0].dcnt))
                return
            for ti in range(NT):
                P.dma("sp", outv[:, :, ti * T:(ti + 1) * T], xT[:, :, ti * T:(ti + 1) * T],
                      reads=[xbuf[c][ti] for c in range(8)], dbuf=xbuf[0][ti])
            for ti in range(NT):
                P._need("sp", (xbuf[0][ti].dsem, xbuf[0][ti].dcnt)) if not P.dry else None

        P.dry = True
        body()
        P.dry = False
        body()
        assert W.i == len(W.plan)
        P.emit()
        print("instr counts:", {e: len(P.q[e]) for e in ENGINES}, "waits", P.nwait, "dma sems", P.ndsem, "scratch cap", scr_cap)
    return nc


def _consts():
    i = np.arange(128)
    r = i[:, None]
    c = i[None, :]
    ident = (r == c).astype(np.float32)
    tri = (r >= c).astype(np.float32)
    ones = np.ones((128, 128), np.float32)
    onesL = np.concatenate([np.ones((128, 64)), np.zeros((128, 64))], 1).astype(np.float32)
    onesR = np.concatenate([np.zeros((128, 64)), np.ones((128, 64))], 1).astype(np.float32)
    Md = np.where(r <= c, 0.0, NEG).astype(np.float32)
    Mp = np.where(c <= r, 0.0, NEG).astype(np.float32)
    mrow = np.concatenate([Mp, Mp, Md, Md], 1)
    zmask = np.where(r >= c, NEG, 0.0).astype(np.float32)
    amask = np.where(r >= c, -NEG, 0.0).astype(np.float32)
    cbf = np.concatenate([ident, tri, ones, onesL, onesR, mrow, zmask, amask, np.zeros((128, 128), np.float32)], 1)
    assert cbf.shape == (128, 1536)
    maskA = (c >= r).astype(np.float32)
    cf = np.zeros((128, 132), np.float32)
    cf[:, 0:128] = maskA
    cf[:, 128] = EPS
    cf[:, 129] = 1.0
    return cbf.astype(ml_dtypes.bfloat16), cf


_NC_CACHE = {}


def _get_nc(stop=None):
    if stop not in _NC_CACHE:
        _NC_CACHE[stop] = build_program(stop=stop)
    return _NC_CACHE[stop]


def make_in_maps(inputs, n_cores=8):
    f = lambda a: np.ascontiguousarray(np.asarray(a, dtype=np.float32))
    x = f(inputs["x"])
    gl = []
    for name in ["ffn1_pre_g", "ffn1_post_g", "mix_pre_g", "mix_post_g", "ffn2_pre_g", "ffn2_post_g"]:
        g = f(inputs[name])
        gl.append(g.reshape(DEPTH, 8, 128).transpose(2, 0, 1).reshape(128, 16))
    gains = np.ascontiguousarray(np.concatenate(gl, 1))
    cbf, cf = _consts()
    ln_g = f(inputs["a_ln_g"]); ln_b = f(inputs["a_ln_b"])
    lnrep = np.ascontiguousarray(np.concatenate([
        np.broadcast_to(ln_g[:, None, :], (DEPTH, 128, 512)),
        np.broadcast_to(ln_b[:, None, :], (DEPTH, 128, 512))], 2))
    bs = f(inputs["a_bs"])
    bsrep = np.ascontiguousarray(np.broadcast_to(bs.reshape(DEPTH, 1, 512), (DEPTH, 128, 512)))
    ws = f(inputs["a_ws"])
    wsT = np.ascontiguousarray(ws.transpose(0, 3, 1, 2).reshape(DEPTH, 128, 512))
    shared = {
        "gains": gains, "cbf": cbf, "cf32": cf, "lnrep": lnrep, "bsrep": bsrep, "wsT": wsT,
        "ffn1_wi": f(inputs["ffn1_wi"]), "ffn2_wi": f(inputs["ffn2_wi"]),
        "ffn1_wo": f(inputs["ffn1_wo"]), "ffn2_wo": f(inputs["ffn2_wo"]),
        "w_in": f(inputs["w_in"]), "w_pa": f(inputs["w_pa"]), "w_pb": f(inputs["w_pb"]),
        "w_pc": f(inputs["w_pc"]), "w_o": f(inputs["w_o"]),
    }
    maps = []
    for b in range(n_cores):
        m = dict(shared)
        m["xT"] = np.ascontiguousarray(x[b].T)
        maps.append(m)
    return maps


def kernel(**inputs):
    nc = _get_nc()
    in_maps = make_in_maps(inputs, 8)
    res = run_bass_kernel_spmd(nc, in_maps, core_ids=list(range(8)))
    outs = [np.asarray(r["outT"]).T for r in res.results]
    return np.ascontiguousarray(np.stack(outs, 0).astype(np.float32))
```

```python
import numpy as np
import ml_dtypes
from contextlib import ExitStack
import concourse.bass as bass
import concourse.mybir as mybir
from concourse.bass_utils import run_bass_kernel_spmd

F32 = mybir.dt.float32
BF16 = mybir.dt.bfloat16
AF = mybir.ActivationFunctionType
ALU = mybir.AluOpType

ENGINES = ("pe", "act", "dve", "pool", "sp")

D = 1024
S = 2048
T = 512
NT = S // T
DFF = 2816
NFC = DFF // 128
DEPTH = 2
QB0, KB0, VB0 = 1024, 1024 + 768, 1024 + 1536
QC0 = 1024 + 2304
KC0, VC0 = QC0 + 512, QC0 + 1024
GATE0 = QC0 + 1536
EPS = 1e-6
NEG = -30000.0
import os
BLEVEL = int(os.environ.get("BLEVEL", "9"))

G_F1PRE, G_F1POST, G_MPRE, G_MPOST, G_F2PRE, G_F2POST = range(6)


class Buf:
    __slots__ = ("name", "w", "r", "dsem", "dcnt")

    def __init__(self, name=""):
        self.name = name
        self.w = None
        self.r = {}
        self.dsem = None
        self.dcnt = 0

    def inherit(self, *olds):
        for o in olds:
            evs = list(o.r.items())
            if o.w is not None:
                evs.append(o.w)
            for k, v in evs:
                if self.r.get(k, 0) < v:
                    self.r[k] = v
        return self


class Prog:
    def __init__(self, nc, stack):
        self.nc = nc
        self.stack = stack
        self.dry = False
        self.q = {e: [] for e in ENGINES}
        self.sem = {}
        self.cnt = {}
        for e in ENGINES:
            self.sem[e] = stack.enter_context(nc.semaphore("s_" + e))
            self.cnt[e] = 0
        self.seen = {e: {} for e in ENGINES}
        self.ndsem = 0
        self.pending = {e: False for e in ENGINES}
        self.nwait = 0

    def _need(self, eng, dep):
        if dep is None:
            return
        key, val = dep
        if key == "pe" and eng == "pe":
            return
        if self.seen[eng].get(key, 0) >= val:
            return
        if key in self.cnt:
            assert val <= self.cnt[key], ("wait on future signal", eng, key, val, self.cnt[key])
        self.seen[eng][key] = val
        sem = self.sem[key]
        self.nwait += 1
        self.q[eng].append(lambda e, sem=sem, val=val: e.wait_ge(sem, val))

    def _deps(self, eng, reads, writes):
        for b in reads:
            self._need(eng, b.w)
        for b in writes:
            self._need(eng, b.w)
            for k, v in list(b.r.items()):
                self._need(eng, (k, v))

    def op(self, eng, fn, reads=(), writes=(), signal=True):
        if self.dry:
            return
        if eng != "pe":
            excl = [b for b in reads if b.name.startswith("bank") and b not in writes]
            if excl:
                writes = list(writes) + excl
        self._deps(eng, reads, writes)
        val = self.cnt[eng] + 1
        ev = (eng, val)
        for b in reads:
            b.r[eng] = val
        for b in writes:
            b.w = ev
            b.r = {}
        if signal:
            self.cnt[eng] = val
            self.pending[eng] = False
            sem = self.sem[eng]
            self.q[eng].append(lambda e, fn=fn, sem=sem: fn(e).then_inc(sem, 1))
        else:
            assert eng == "pe"
            self.pending[eng] = True
            self.q[eng].append(lambda e, fn=fn: fn(e))

    def dma(self, eng, out, in_, reads=(), writes=(), dbuf=None):
        if self.dry:
            return
        self._deps(eng, reads, writes)
        if dbuf is None:
            dbuf = writes[0] if writes else reads[0]
        if dbuf.dsem is None:
            key = "d%d" % self.ndsem
            self.ndsem += 1
            self.sem[key] = self.stack.enter_context(self.nc.semaphore("s_" + key))
            dbuf.dsem = key
        dbuf.dcnt += 16
        ev = (dbuf.dsem, dbuf.dcnt)
        for b in reads:
            b.r[dbuf.dsem] = dbuf.dcnt
        for b in writes:
            b.w = ev
            b.r = {}
        sem = self.sem[dbuf.dsem]
        self.q[eng].append(lambda e, out=out, in_=in_, sem=sem: e.dma_start(out=out, in_=in_).then_inc(sem, 16))

    def barrier(self):
        if self.dry:
            return
        engs = ("pe", "act", "dve", "pool")
        for e in engs:
            for o in engs:
                if o != e and self.cnt[o] > 0:
                    self._need(e, (o, self.cnt[o]))

    def emit(self):
        nc = self.nc
        assert not any(self.pending.values()), self.pending
        with nc.Block() as block:
            @block.tensor
            def _(e):
                for f in self.q["pe"]:
                    f(e)

            @block.scalar
            def _(e):
                for f in self.q["act"]:
                    f(e)

            @block.vector
            def _(e):
                for f in self.q["dve"]:
                    f(e)

            @block.gpsimd
            def _(e):
                for f in self.q["pool"]:
                    f(e)

            @block.sync
            def _(e):
                for f in self.q["sp"]:
                    f(e)


class SubArena:
    def __init__(self, t, base, cap):
        self.t = t
        self.base = base
        self.cap = cap
        self.off = 0

    def reset(self):
        self.off = 0

    def _alloc(self, nbytes):
        off = (self.off + 63) // 64 * 64
        assert off + nbytes <= self.cap, ("arena overflow", off, nbytes, self.cap)
        self.off = off + nbytes
        return self.base + off

    def bf(self, nelem):
        o = self._alloc(nelem * 2)
        return self.t[:, o // 2: o // 2 + nelem]

    def f32(self, nelem):
        o = self._alloc(nelem * 4)
        return self.t[:, o // 2: o // 2 + 2 * nelem].bitcast(F32)


class WStream:
    def __init__(self, P, classes):
        self.P = P
        self.cls = classes
        self.plan = []
        self.i = 0
        self.next_emit = 0
        self.done = {c: 0 for c in classes}
        self.num = {c: 0 for c in classes}

    def _cls_for(self, nelem):
        best = None
        for c, slots in self.cls.items():
            cap = slots[0][0].shape[1]
            if nelem <= cap and (best is None or cap < self.cls[best][0][0].shape[1]):
                best = c
        assert best is not None, nelem
        return best

    def get(self, src, nk, ncols):
        return self.get_many([(src, nk, ncols)])[0]

    def get_many(self, specs):
        P = self.P
        outs = []
        if P.dry:
            for src, nk, ncols in specs:
                nelem = nk * ncols
                c = self._cls_for(nelem)
                assert len(specs) <= len(self.cls[c])
                self.plan.append((src, nk, ncols, c, self.num[c]))
                self.num[c] += 1
                view, buf = self.cls[c][0]
                outs.append((view[:, 0:nelem].rearrange("p (k n) -> p k n", k=nk), Buf()))
            return outs
        last = self.i + len(specs) - 1
        while self.next_emit < len(self.plan):
            s2, nk2, nc2, c2, k2 = self.plan[self.next_emit]
            n2 = len(self.cls[c2])
            if k2 - self.done[c2] > n2 - 1:
                break
            view2, buf2 = self.cls[c2][k2 % n2]
            dst = view2[:, 0:nk2 * nc2].rearrange("p (k n) -> p k n", k=nk2)
            P.dma("pool", dst, s2, writes=[buf2])
            self.next_emit += 1
        assert self.next_emit > last, ("slab not emitted", self.i)
        for src, nk, ncols in specs:
            src_, nk_, ncols_, c, k = self.plan[self.i]
            assert (nk_, ncols_) == (nk, ncols), ("wstream mismatch", self.i)
            n = len(self.cls[c])
            view, buf = self.cls[c][k % n]
            outs.append((view[:, 0:nk * ncols].rearrange("p (k n) -> p k n", k=nk), buf))
            self.i += 1
        for src, nk, ncols in specs:
            c = self._cls_for(nk * ncols)
            self.done[c] += 1
        return outs


class LazyDram:
    def __init__(self, nc, name, shape, dtype=F32):
        self.nc, self.name, self.shape, self.dtype, self._ap = nc, name, shape, dtype, None

    def __getitem__(self, idx):
        if self._ap is None:
            self._ap = self.nc.dram_tensor(self.name, list(self.shape), self.dtype, kind="ExternalInput").ap()
            DECLARED.append(self.name)
        return self._ap[idx]


DECLARED = []


def build_program(stop=None, plan=None, parts="BCE", dump=None):
    nc = bass.Bass("TRN2", target_bir_lowering=False)
    del DECLARED[:]

    def dram(n, s, d=F32, kind="ExternalInput"):
        if kind == "ExternalInput":
            DECLARED.append(n)
        return nc.dram_tensor(n, list(s), d, kind=kind).ap()
    xin = dram("xT", [D, S])
    out = dram("outT", [D, S], kind="ExternalOutput")
    gains_d = dram("gains", [128, 96])
    cbf_d = dram("cbf", [128, 1536], BF16)
    cf_d = dram("cf32", [128, 132])
    lnrep_d = dram("lnrep", [DEPTH, 128, 1024])
    bsrep_d = dram("bsrep", [DEPTH, 128, 512])
    wsT_d = dram("wsT", [DEPTH, 128, 512])
    ffn_wi = [LazyDram(nc, "ffn1_wi", [DEPTH, D, 2 * DFF]), LazyDram(nc, "ffn2_wi", [DEPTH, D, 2 * DFF])]
    ffn_wo = [LazyDram(nc, "ffn1_wo", [DEPTH, DFF, D]), LazyDram(nc, "ffn2_wo", [DEPTH, DFF, D])]
    w_in = LazyDram(nc, "w_in", [DEPTH, D, 7936])
    w_pa = LazyDram(nc, "w_pa", [DEPTH, 512, D])
    w_pb = LazyDram(nc, "w_pb", [DEPTH, 256, D])
    w_pc = LazyDram(nc, "w_pc", [DEPTH, 512, D])
    w_o = LazyDram(nc, "w_o", [DEPTH, D, D])

    def wview(w2d):
        return w2d.rearrange("(kc p) n -> p kc n", p=128)

    with ExitStack() as st:
        P = Prog(nc, st)
        total = int(nc.sbuf_bytes_remaining) // 64 * 64 - 256
        arena_t = st.enter_context(nc.sbuf_tensor("arena", [128, total // 2], BF16))
        main = SubArena(arena_t, 0, total)
        ps = st.enter_context(nc.psum_tensor("ps", [128, 8, 512], F32))
        pbuf = [Buf("bank%d" % i) for i in range(8)]
        free_banks = list(range(8))

        def bank():
            return free_banks.pop(0)

        def release(b):
            free_banks.append(b)

        xT = main.f32(8 * S).rearrange("p (c t) -> p c t", c=8)
        xbuf = [[Buf("x%d_%d" % (c, ti)) for ti in range(NT)] for c in range(8)]
        hT = main.bf(8 * S).rearrange("p (c t) -> p c t", c=8)
        hbuf = [Buf("h%d" % ti) for ti in range(NT)]
        gains = main.f32(96); bgains = Buf("gains")
        ghalf = main.f32(96); bghalf = Buf("ghalf")
        cbf = main.bf(1536); bcbf = Buf("cbf")
        cf = main.f32(132); bcf = Buf("cf")
        lnrep = main.f32(1024); blnrep = Buf("lnrep")
        bsrep = main.f32(512); bbsrep = Buf("bsrep")
        wsT = main.bf(512); bwsT = Buf("wsT")
        NSMALL, NBIG = 5, 2
        small = [(main.bf(2048), Buf("ws%d" % i)) for i in range(NSMALL)]
        big = [(main.bf(NFC * 128), Buf("wb%d" % i)) for i in range(NBIG)]
        W = WStream(P, {"s": small, "b": big})
        scr_cap = total - ((main.off + 63) // 64 * 64)
        scr = SubArena(arena_t, (main.off + 63) // 64 * 64, scr_cap)
        scr_users = []

        ident = cbf[:, 0:128]
        tri = cbf[:, 128:256]
        ones = cbf[:, 256:384]
        onesL = cbf[:, 384:512]
        onesR = cbf[:, 512:640]
        mrow = cbf[:, 640:1152]
        zmask = cbf[:, 1152:1280]
        amask = cbf[:, 1280:1408]
        maskA = cf[:, 0:128]
        eps_ap = cf[:, 128:129]
        one_ap = cf[:, 129:130]

        def gcol(gi, l, c):
            o = gi * 16 + l * 8 + c
            return gains[:, o:o + 1]

        def ghcol(gi, l, c):
            o = gi * 16 + l * 8 + c
            return ghalf[:, o:o + 1]

        def mm(out_, lhsT, rhs, start, stop, reads, writes, signal=True):
            P.op("pe", lambda e: e.matmul(out_, lhsT, rhs, start=start, stop=stop), reads=reads, writes=writes, signal=signal)

        dump_src = {}

        def new_scratch(bufs):
            for b in bufs:
                b.inherit(*scr_users)
            return bufs

        def body():
            nonlocal scr_users
            free_banks[:] = list(range(8))
            P.dma("sp", gains, gains_d, writes=[bgains])
            P.dma("sp", cbf, cbf_d, writes=[bcbf])
            P.dma("sp", cf, cf_d, writes=[bcf])
            for ti in range(NT):
                P.dma("sp", xT[:, :, ti * T:(ti + 1) * T], xin.rearrange("(c p) t -> p c t", p=128)[:, :, ti * T:(ti + 1) * T],
                      writes=[xbuf[c][ti] for c in range(8)], dbuf=xbuf[0][ti])
            P.op("dve", lambda e: e.tensor_scalar(ghalf, gains, 0.5, None, ALU.mult), reads=[bgains], writes=[bghalf])

            def rstd_from_bank(bss, dst, dstbuf):
                P.op("act", lambda e: e.activation(dst, ps[:, bss, :], AF.Sqrt, bias=eps_ap, scale=1.0 / D),
                     reads=[pbuf[bss], bcf], writes=[dstbuf])
                P.op("dve", lambda e: e.reciprocal(dst, dst), reads=[dstbuf], writes=[dstbuf])

            def prenorm(l, gi, ti, sq, bsq, rstd, brstd):
                tok = slice(ti * T, (ti + 1) * T)
                bss = bank()
                for c in range(8):
                    k = c % 2
                    P.op("act", lambda e, c=c, k=k: e.activation(sq[k], xT[:, c, tok], AF.Square),
                         reads=[xbuf[c][ti]], writes=[bsq[k]])
                    mm(ps[:, bss, :], ones, sq[k], c == 0, c == 7, [bsq[k], bcbf], [pbuf[bss]])
                rstd_from_bank(bss, rstd, brstd)
                release(bss)
                for c in range(8):
                    P.op("dve", lambda e, c=c: e.scalar_tensor_tensor(hT[:, c, tok], xT[:, c, tok], gcol(gi, l, c), rstd, ALU.mult, ALU.mult),
                         reads=[xbuf[c][ti], brstd, bgains], writes=[hbuf[ti]])

            def prenorm_a(ti, sq8, bsq8):
                tok = slice(ti * T, (ti + 1) * T)
                for c in range(8):
                    P.op("act", lambda e, c=c: e.activation(sq8[c], xT[:, c, tok], AF.Square),
                         reads=[xbuf[c][ti]], writes=[bsq8[c]])

            def prenorm_b(l, gi, ti, sq8, bsq8, rstd, brstd):
                tok = slice(ti * T, (ti + 1) * T)
                bss = bank()
                for c in range(8):
                    mm(ps[:, bss, :], ones, sq8[c], c == 0, c == 7, [bsq8[c], bcbf], [pbuf[bss]], signal=(c == 7))
                rstd_from_bank(bss, rstd, brstd)
                release(bss)
                for c in range(8):
                    P.op("dve", lambda e, c=c: e.scalar_tensor_tensor(hT[:, c, tok], xT[:, c, tok], gcol(gi, l, c), rstd, ALU.mult, ALU.mult),
                         reads=[xbuf[c][ti], brstd, bgains], writes=[hbuf[ti]])

            def postnorm_residual(l, gi, ti, ybuf, bybuf, bss, rstd, brstd, tmp, btmp, half):
                tok = slice(ti * T, (ti + 1) * T)
                rstd_from_bank(bss, rstd, brstd)
                release(bss)
                for c in range(8):
                    k = c % 2
                    sc = ghcol(gi, l, c) if half else gcol(gi, l, c)
                    P.op("dve", lambda e, c=c, k=k, sc=sc: e.scalar_tensor_tensor(tmp[k], ybuf[:, c, :], sc, rstd, ALU.mult, ALU.mult),
                         reads=[bybuf[c], brstd, bghalf, bgains], writes=[btmp[k]])
                    P.op("pool", lambda e, c=c, k=k: e.tensor_tensor(xT[:, c, tok], xT[:, c, tok], tmp[k], ALU.add),
                         reads=[btmp[k], xbuf[c][ti]], writes=[xbuf[c][ti]])

            def ffn(l, which):
                nonlocal scr_users
                scr.reset()
                gi_pre = G_F1PRE if which == 0 else G_F2PRE
                gi_post = G_F1POST if which == 0 else G_F2POST
                wi = wview(ffn_wi[which][l])
                wo = wview(ffn_wo[which][l])
                act = scr.bf(NFC * T).rearrange("p (c t) -> p c t", c=NFC)
                bact = [Buf("act%d" % j) for j in range(NFC)]
                ybuf = scr.f32(8 * T).rearrange("p (c t) -> p c t", c=8)
                bybuf = [Buf("y%d" % c) for c in range(8)]
                sq = [scr.bf(T) for _ in range(2)]; bsq = [Buf("sq%d" % i) for i in range(2)]
                stmp = [scr.f32(T) for _ in range(2)]; bstmp = [Buf("st%d" % i) for i in range(2)]
                rstd = scr.f32(T); brstd = Buf("rstd")
                tmp = [scr.f32(T) for _ in range(2)]; btmp = [Buf("tmp%d" % i) for i in range(2)]
                sq8 = [scr.bf(T) for _ in range(8)]; bsq8 = [Buf("sq8_%d" % i) for i in range(8)]
                rstdp = scr.f32(T); brstdp = Buf("rstdp")
                mine = bact + bybuf + bsq + bstmp + [brstd] + btmp + bsq8 + [brstdp]
                new_scratch(mine)
                prenorm_a(0, sq8, bsq8)
                prenorm_b(l, gi_pre, 0, sq8, bsq8, rstdp, brstdp)
                for ti in range(NT):
                    tok = slice(ti * T, (ti + 1) * T)
                    for jp in range(NFC // 2):
                        if ti + 1 < NT and jp == 4:
                            prenorm_a(ti + 1, sq8, bsq8)
                        if ti + 1 < NT and jp == 8:
                            prenorm_b(l, gi_pre, ti + 1, sq8, bsq8, rstdp, brstdp)
                        (gs, bgs), (us, bus) = W.get_many([(wi[:, :, jp * 256:(jp + 1) * 256], 8, 256),
                                                           (wi[:, :, DFF + jp * 256:DFF + (jp + 1) * 256], 8, 256)])
                        for jj in range(2):
                            j = jp * 2 + jj
                            bg = bank(); bu = bank()
                            for kc in range(8):
                                mm(ps[:, bg, :], gs[:, kc, jj * 128:(jj + 1) * 128], hT[:, kc, tok], kc == 0, kc == 7,
                                   [bgs, hbuf[ti]], [pbuf[bg]], signal=(kc == 7))
                            for kc in range(8):
                                mm(ps[:, bu, :], us[:, kc, jj * 128:(jj + 1) * 128], hT[:, kc, tok], kc == 0, kc == 7,
                                   [bus, hbuf[ti]], [pbuf[bu]], signal=(kc == 7))
                            k = j % 2
                            P.op("act", lambda e, k=k, bg=bg: e.activation(stmp[k], ps[:, bg, :], AF.Silu),
                                 reads=[pbuf[bg]], writes=[bstmp[k]])
                            P.op("dve", lambda e, k=k, bu=bu, j=j: e.tensor_tensor(act[:, j, :], stmp[k], ps[:, bu, :], ALU.mult),
                                 reads=[bstmp[k], pbuf[bu]], writes=[bact[j]])
                            release(bg); release(bu)
                    bss = bank()
                    for m in range(8):
                        ws_, bws = W.get(wo[:, :, m * 128:(m + 1) * 128], NFC, 128)
                        b = bank()
                        for fc in range(NFC):
                            mm(ps[:, b, :], ws_[:, fc, :], act[:, fc, :], fc == 0, fc == NFC - 1,
                               [bws, bact[fc]], [pbuf[b]], signal=(fc == NFC - 1))
                        P.op("act", lambda e, m=m, b=b: e.activation(ybuf[:, m, :], ps[:, b, :], AF.Copy),
                             reads=[pbuf[b]], writes=[bybuf[m]])
                        k = m % 2
                        P.op("act", lambda e, k=k, b=b: e.activation(sq[k], ps[:, b, :], AF.Square),
                             reads=[pbuf[b]], writes=[bsq[k]])
                        release(b)
                        if m > 0:
                            mm(ps[:, bss, :], ones, sq[(m - 1) % 2], m == 1, False, [bsq[(m - 1) % 2], bcbf], [pbuf[bss]])
                    mm(ps[:, bss, :], ones, sq[7 % 2], False, True, [bsq[7 % 2], bcbf], [pbuf[bss]])
                    postnorm_residual(l, gi_post, ti, ybuf, bybuf, bss, rstd, brstd, tmp, btmp, True)
                scr_users = mine

            def mixer(l):
                nonlocal scr_users
                scr.reset()
                win = wview(w_in[l])
                P.dma("sp", lnrep, lnrep_d[l], writes=[blnrep])
                P.dma("sp", bsrep, bsrep_d[l], writes=[bbsrep])
                yb = scr.bf(2 * S).rearrange("p (c t) -> p c t", c=2)
                byb = [Buf("yb%d" % j) for j in range(2)]
                yc = scr.bf(4 * S).rearrange("p (c t) -> p c t", c=4)
                byc = [[Buf("yc%d_%d" % (j, ti)) for ti in range(NT)] for j in range(4)]
                keep = byb + [b for row in byc for b in row]
                new_scratch(keep)
                dump_src["yb"] = (yb, byb, 2)
                dump_src["yc"] = (yc, [b for row in byc for b in row], 4)
                scr_mark = scr.off
                sq = [scr.bf(T) for _ in range(2)]; bsq = [Buf() for _ in range(2)]
                rstd = scr.f32(T); brstd = Buf()
                wtmp = scr.f32(512); bwtmp = Buf()
                pre = new_scratch(bsq + [brstd, bwtmp])
                P.dma("sp", wtmp, wsT_d[l], writes=[bwtmp])
                for g in range(4):
                    P.op("dve", lambda e, g=g: e.tensor_tensor(wsT[:, g * 128:(g + 1) * 128], wtmp[:, g * 128:(g + 1) * 128], maskA, ALU.mult),
                         reads=[bwtmp, bcf], writes=[bwsT])
                for ti in range(NT):
                    prenorm(l, G_MPRE, ti, sq, bsq, rstd, brstd)
                scr_users = keep + pre

                scr.off = scr_mark
                P.barrier()
                if "B" in parts:
                    qpz = [scr.bf(S) for _ in range(2)]; kperm = scr.bf(S)
                    bqp = Buf("qperm"); bkp = Buf("kperm")
                    vz = scr.bf(16 * 2 * 128).rearrange("p (b h d) -> p b h d", b=16, h=2)
                    bvz = [Buf("vz%d" % i) for i in range(4)]
                    numacc = scr.f32(S); denacc = scr.f32(S)
                    bnum = Buf("num"); bden = Buf("den")
                    pT = [scr.bf(512) for _ in range(3)]; bpT = [Buf() for _ in range(3)]
                    mineB = new_scratch([bqp, bkp] + bvz + [bnum, bden] + bpT)
                    for i4 in range(4):
                        P.op("pool", lambda e, i4=i4: e.memset(vz[:, i4 * 4:(i4 + 1) * 4], 0.0), writes=[bvz[i4]])
                    for hh in range(2):
                        P.op("pool", lambda e, hh=hh: e.memset(qpz[hh], 0.0), writes=[bqp])
                    pcount = 0
                    for j in range(2):
                        for g in range(3):
                            r = (1, 4, 16)[g]
                            Lc = S // r
                            nblk = Lc // 128
                            (sq_, bsq_), (sk_, bsk_), (sv_, bsv_) = W.get_many([
                                (win[:, :, QB0 + g * 256 + j * 128: QB0 + g * 256 + (j + 1) * 128], 8, 128),
                                (win[:, :, KB0 + g * 256 + j * 128: KB0 + g * 256 + (j + 1) * 128], 8, 128),
                                (win[:, :, VB0 + g * 256 + j * 128: VB0 + g * 256 + (j + 1) * 128], 8, 128)])
                            qp3 = [qpz[hh].rearrange("p (c i) -> p c i", c=r) for hh in range(2)]
                            kp3 = kperm.rearrange("p (c i) -> p c i", c=r)
                            w = T // r
                            for ti in range(NT):
                                tok = slice(ti * T, (ti + 1) * T)
                                bq = bank(); bk = bank()
                                for kc in range(8):
                                    mm(ps[:, bq, :], sq_[:, kc, :], hT[:, kc, tok], kc == 0, kc == 7, [bsq_, hbuf[ti]], [pbuf[bq]], signal=(kc == 7))
                                for kc in range(8):
                                    mm(ps[:, bk, :], sk_[:, kc, :], hT[:, kc, tok], kc == 0, kc == 7, [bsk_, hbuf[ti]], [pbuf[bk]], signal=(kc == 7))
                                for hh in range(2):
                                    hs = slice(hh * 64, (hh + 1) * 64)
                                    P.op("act", lambda e, bq=bq, ti=ti, qp3=qp3, r=r, w=w, hh=hh, hs=hs: e.activation(
                                        qp3[hh][hs, :, ti * w:(ti + 1) * w], ps[hs, bq, :].rearrange("p (i c) -> p c i", c=r), AF.Copy, scale=0.125),
                                        reads=[pbuf[bq]], writes=[bqp])
                                P.op("dve", lambda e, bk=bk, ti=ti, kp3=kp3, r=r, w=w: e.tensor_copy(
                                    kp3[:, :, ti * w:(ti + 1) * w], ps[:, bk, :].rearrange("p (i c) -> p c i", c=r)),
                                    reads=[pbuf[bk]], writes=[bkp])
                                release(bq); release(bk)
                            for b4 in range(4):
                                bv = bank()
                                for bb in range(4):
                                    bp = b4 * 4 + bb
                                    p0 = bp * 128
                                    c = p0 // Lc
                                    i0 = p0 % Lc
                                    t0 = c + r * i0
                                    for kc in range(8):
                                        mm(ps[:, bv, bb * 128:(bb + 1) * 128], hT[:, kc, t0:t0 + r * 127 + 1:r], sv_[:, kc, :],
                                           kc == 0, kc == 7, [bsv_] + hbuf, [pbuf[bv]], signal=(kc == 7))
                                pv = ps[:, bv, :].rearrange("p (b d) -> p b d", b=4)
                                P.op("dve", lambda e, b4=b4, pv=pv: e.tensor_copy(vz[:, b4 * 4:(b4 + 1) * 4, 0, 0:64], pv[:, :, 0:64]),
                                     reads=[pbuf[bv]], writes=[bvz[b4]])
                                P.op("act", lambda e, b4=b4, pv=pv: e.activation(vz[:, b4 * 4:(b4 + 1) * 4, 1, 64:128], pv[:, :, 64:128], AF.Copy),
                                     reads=[pbuf[bv]], writes=[bvz[b4]])
                                release(bv)
                            def b_stage1(c, qb):
                                nonlocal pcount
                                bpq = c * nblk + qb
                                kbs = ([bpq - 1] if qb > 0 else []) + [bpq]
                                nk = len(kbs)
                                Wd = nk * 256
                                bs_ = bank()
                                mm(ps[:, bs_, 0:Wd], ident, mrow[:, 512 - Wd:512], True, False, [bcbf], [pbuf[bs_]], signal=False)
                                for a, kb in enumerate(kbs):
                                    for hh in range(2):
                                        last = (a == nk - 1 and hh == 1)
                                        mm(ps[:, bs_, a * 256 + hh * 128: a * 256 + (hh + 1) * 128],
                                           kperm[:, kb * 128:(kb + 1) * 128],
                                           qpz[hh][:, bpq * 128:(bpq + 1) * 128],
                                           False, last, [bqp, bkp], [pbuf[bs_]], signal=last)
                                k = pcount % 3
                                pcount += 1
                                P.op("act", lambda e, k=k, bs_=bs_, Wd=Wd: e.activation(pT[k][:, 0:Wd], ps[:, bs_, 0:Wd], AF.Exp),
                                     reads=[pbuf[bs_]], writes=[bpT[k]])
                                release(bs_)
                                return (c, qb, kbs, nk, k)

                            def b_stage2(c, qb, kbs, nk, k):
                                bo = bank()
                                n = 0
                                for a, kb in enumerate(kbs):
                                    for hh in range(2):
                                        n += 1
                                        mm(ps[:, bo, 0:128], vz[:, kb, hh, :], pT[k][:, a * 256 + hh * 128: a * 256 + (hh + 1) * 128],
                                           n == 1, n == 2 * nk, [bvz[kb // 4], bpT[k]], [pbuf[bo]], signal=(n == 2 * nk))
                                n = 0
                                for a, kb in enumerate(kbs):
                                    for hh in range(2):
                                        n += 1
                                        mm(ps[:, bo, 128:256], onesL if hh == 0 else onesR, pT[k][:, a * 256 + hh * 128: a * 256 + (hh + 1) * 128],
                                           n == 1, n == 2 * nk, [bcbf, bpT[k]], [pbuf[bo]], signal=(n == 2 * nk))
                                cols = slice(c + r * qb * 128, c + r * qb * 128 + r * 127 + 1, r)
                                if g == 0:
                                    P.op("dve", lambda e, bo=bo, cols=cols: e.tensor_copy(numacc[:, cols], ps[:, bo, 0:128]),
                                         reads=[pbuf[bo]], writes=[bnum])
                                    P.op("act", lambda e, bo=bo, cols=cols: e.activation(denacc[:, cols], ps[:, bo, 128:256], AF.Copy),
                                         reads=[pbuf[bo]], writes=[bden])
                                else:
                                    P.op("dve", lambda e, bo=bo, cols=cols: e.tensor_tensor(numacc[:, cols], numacc[:, cols], ps[:, bo, 0:128], ALU.add),
                                         reads=[pbuf[bo]], writes=[bnum])
                                    P.op("dve", lambda e, bo=bo, cols=cols: e.tensor_tensor(denacc[:, cols], denacc[:, cols], ps[:, bo, 128:256], ALU.add),
                                         reads=[pbuf[bo]], writes=[bden])
                                release(bo)

                            b_items = [(c, qb) for c in range(r) for qb in range(nblk)]
                            prev = None
                            for it in b_items:
                                cur = b_stage1(*it)
                                if prev is not None:
                                    b_stage2(*prev)
                                prev = cur
                            b_stage2(*prev)
                        if BLEVEL < 4:
                            P.op("dve", lambda e: e.memset(denacc, 1.0), writes=[bden])
                            P.op("dve", lambda e: e.memset(numacc, 1.0), writes=[bnum])
                        P.op("dve", lambda e: e.reciprocal(denacc, denacc), reads=[bden], writes=[bden])
                        P.op("dve", lambda e, j=j: e.tensor_tensor(yb[:, j, :], numacc, denacc, ALU.mult), reads=[bnum, bden], writes=[byb[j]])
                    scr_users = keep + mineB

                scr.off = scr_mark
                P.barrier()
                if "C" in parts:
                    qz = [scr.bf(S) for _ in range(2)]; kT = scr.bf(S); nkT = scr.bf(S)
                    bqT = Buf("qT"); bkT = Buf("kT"); bnkT = Buf("nkT")
                    vzc = scr.bf(16 * 2 * 128).rearrange("p (b h d) -> p b h d", b=16, h=2)
                    bvzc = [Buf("cvz%d" % i) for i in range(4)]
                    NR = 3
                    ebuf = [scr.f32(T) for _ in range(NR)]; beb = [Buf() for _ in range(NR)]
                    Lbuf = [scr.bf(T) for _ in range(NR)]; bLb = [Buf() for _ in range(NR)]
                    Abuf = [scr.bf(T) for _ in range(NR)]; bAb = [Buf() for _ in range(NR)]
                    Rb = [scr.bf(T) for _ in range(2)]; bRb = [Buf() for _ in range(2)]
                    mineC = new_scratch([bqT, bkT, bnkT] + bvzc + beb + bLb + bAb + bRb)
                    for i4 in range(4):
                        P.op("pool", lambda e, i4=i4: e.memset(vzc[:, i4 * 4:(i4 + 1) * 4], 0.0), writes=[bvzc[i4]])
                    for hh in range(2):
                        P.op("pool", lambda e, hh=hh: e.memset(qz[hh], 0.0), writes=[bqT])
                    itc = 0
                    for j in range(4):
                        (sq_, bsq_), (sk_, bsk_), (sv_, bsv_) = W.get_many([
                            (win[:, :, QC0 + j * 128: QC0 + (j + 1) * 128], 8, 128),
                            (win[:, :, KC0 + j * 128: KC0 + (j + 1) * 128], 8, 128),
                            (win[:, :, VC0 + j * 128: VC0 + (j + 1) * 128], 8, 128)])
                        for ti in range(NT):
                            tok = slice(ti * T, (ti + 1) * T)
                            bq = bank(); bk = bank()
                            for kc in range(8):
                                mm(ps[:, bq, :], sq_[:, kc, :], hT[:, kc, tok], kc == 0, kc == 7, [bsq_, hbuf[ti]], [pbuf[bq]], signal=(kc == 7))
                            for kc in range(8):
                                mm(ps[:, bk, :], sk_[:, kc, :], hT[:, kc, tok], kc == 0, kc == 7, [bsk_, hbuf[ti]], [pbuf[bk]], signal=(kc == 7))
                            for hh in range(2):
                                hs = slice(hh * 64, (hh + 1) * 64)
                                P.op("act", lambda e, bq=bq, tok=tok, hh=hh, hs=hs: e.activation(qz[hh][hs, tok], ps[hs, bq, :], AF.Copy, scale=0.125),
                                     reads=[pbuf[bq]], writes=[bqT])
                            P.op("dve", lambda e, bk=bk, tok=tok: e.tensor_copy(kT[:, tok], ps[:, bk, :]), reads=[pbuf[bk]], writes=[bkT])
                            P.op("act", lambda e, bk=bk, tok=tok: e.activation(nkT[:, tok], ps[:, bk, :], AF.Copy, scale=-1.0),
                                 reads=[pbuf[bk]], writes=[bnkT])
                            release(bq); release(bk)
                        for b4 in range(4):
                            bv = bank()
                            for bb in range(4):
                                blk = b4 * 4 + bb
                                for kc in range(8):
                                    mm(ps[:, bv, bb * 128:(bb + 1) * 128], hT[:, kc, blk * 128:(blk + 1) * 128], sv_[:, kc, :],
                                       kc == 0, kc == 7, [bsv_, hbuf[blk // 4]], [pbuf[bv]], signal=(kc == 7))
                            pv = ps[:, bv, :].rearrange("p (b d) -> p b d", b=4)
                            P.op("dve", lambda e, b4=b4, pv=pv: e.tensor_copy(vzc[:, b4 * 4:(b4 + 1) * 4, 0, 0:64], pv[:, :, 0:64]),
                                 reads=[pbuf[bv]], writes=[bvzc[b4]])
                            P.op("act", lambda e, b4=b4, pv=pv: e.activation(vzc[:, b4 * 4:(b4 + 1) * 4, 1, 64:128], pv[:, :, 64:128], AF.Copy),
                                 reads=[pbuf[bv]], writes=[bvzc[b4]])
                            release(bv)
                        for ti in range(NT):
                            bo = bank()
                            nsteps = 4 * ti + 4
                            items = [(hh, step) for step in range(nsteps) for hh in range(2)]
                            nit = len(items)
                            state = {}
                            for hh in range(2):
                                P.op("pool", lambda e, hh=hh: e.memset(Rb[hh], 0.0), writes=[bRb[hh]])

                            def info(n):
                                hh, step = items[n]
                                sb = 4 * ti + 3 - step
                                a = sb - 4 * ti
                                col0 = 128 * a if a > 0 else 0
                                return hh, step, sb, a >= 0, col0

                            zstate = {}

                            def S1a(n):
                                hh, step, sb, diag, col0 = info(n)
                                bz = bank()
                                zstate[n] = bz
                                mm(ps[:, bz, col0:T], kT[:, sb * 128:(sb + 1) * 128], qz[hh][:, ti * T + col0:(ti + 1) * T],
                                   True, not diag, [bkT, bqT], [pbuf[bz]], signal=True)
                                if diag:
                                    mm(ps[:, bz, col0:col0 + 128], ident, zmask, False, True, [bcbf], [pbuf[bz]], signal=True)

                            def S1b(n):
                                hh, step, sb, diag, col0 = info(n)
                                k = (itc + n) % NR
                                bz = zstate.pop(n)
                                P.op("act", lambda e: e.activation(ebuf[k][:, col0:T], ps[:, bz, col0:T], AF.Exp),
                                     reads=[pbuf[bz]], writes=[beb[k]])
                                release(bz)

                            def S2(n):
                                hh, step, sb, diag, col0 = info(n)
                                k = (itc + n) % NR
                                hs = slice(hh * 64, (hh + 1) * 64)
                                P.op("act", lambda e: e.activation(Lbuf[k][:, col0:T], ebuf[k][:, col0:T], AF.Ln, bias=one_ap),
                                     reads=[beb[k], bcf], writes=[bLb[k]])
                                ba = bank()
                                state[n] = ba
                                mm(ps[:, ba, col0:T], tri, Lbuf[k][:, col0:T], True, False, [bcbf, bLb[k]], [pbuf[ba]], signal=False)
                                if step > 0:
                                    mm(ps[:, ba, col0:T], ones, Rb[hh][:, col0:T], False, False, [bcbf, bRb[hh]], [pbuf[ba]], signal=False)
                                mm(ps[:, ba, col0:T], nkT[:, sb * 128:(sb + 1) * 128], qz[hh][:, ti * T + col0:(ti + 1) * T],
                                   False, not diag, [bnkT, bqT], [pbuf[ba]], signal=True)
                                if diag:
                                    mm(ps[:, ba, col0:col0 + 128], ident, amask, False, True, [bcbf], [pbuf[ba]], signal=True)
                                if step < nsteps - 1:
                                    P.op("pool", lambda e: e.tensor_tensor(Rb[hh][:, col0:T], Rb[hh][:, col0:T], Lbuf[k][:, col0:T], ALU.add),
                                         reads=[bLb[k]], writes=[bRb[hh]])

                            def S3(n):
                                hh, step, sb, diag, col0 = info(n)
                                k = (itc + n) % NR
                                ba = state.pop(n)
                                P.op("act", lambda e: e.activation(Abuf[k][:, col0:T], ps[:, ba, col0:T], AF.Exp, scale=-1.0),
                                     reads=[pbuf[ba]], writes=[bAb[k]])
                                release(ba)
                                mm(ps[:, bo, col0:T], vzc[:, sb, hh, :], Abuf[k][:, col0:T], n == 0, n == nit - 1,
                                   [bvzc[sb // 4], bAb[k]], [pbuf[bo]], signal=True)

                            for n in range(nit + 2):
                                if n < nit:
                                    S1a(n)
                                if 0 <= n - 1 < nit:
                                    S2(n - 1)
                                if 0 <= n - 2 < nit:
                                    S3(n - 2)
                                if n < nit:
                                    S1b(n)
                            itc += nit
                            tok = slice(ti * T, (ti + 1) * T)
                            P.op("dve", lambda e, bo=bo, j=j, tok=tok: e.tensor_copy(yc[:, j, tok], ps[:, bo, :]), reads=[pbuf[bo]], writes=[byc[j][ti]])
                            release(bo)
                    scr_users = keep + mineC

                scr.off = scr_mark
                P.barrier()
                if "E" in parts:
                    merged = scr.bf(8 * T).rearrange("p (c t) -> p c t", c=8); bmer = [Buf() for _ in range(8)]
                    obuf = scr.f32(8 * T).rearrange("p (c t) -> p c t", c=8); bob = [Buf() for _ in range(8)]
                    uT = scr.bf(4 * T).rearrange("p (c t) -> p c t", c=4); buT = [Buf() for _ in range(4)]
                    yaT = scr.bf(4 * T).rearrange("p (c t) -> p c t", c=4); byaT = Buf()
                    vtmp = scr.f32(T); bvt = Buf()
                    vln = scr.bf(T); bvln = Buf()
                    svt = vtmp; bsvt = bvt
                    gsb = [scr.f32(T) for _ in range(3)]; bgsb = [Buf() for _ in range(3)]
                    sqe = [scr.bf(T) for _ in range(2)]; bsqe = [Buf() for _ in range(2)]
                    rstd = gsb[2]; brstd = bgsb[2]
                    tmp = gsb[0:2]; btmp = bgsb[0:2]
                    st6 = scr.f32(8); bst6 = Buf()
                    mv = scr.f32(4); bmv = Buf()
                    mineE = new_scratch(bmer + bob + buT + [byaT, bvt, bvln] + bgsb + bsqe + [bst6, bmv])
                    lng = lnrep[:, 0:512]
                    lnb = lnrep[:, 512:1024]
                    wpa = wview(w_pa[l]); wpb = wview(w_pb[l]); wpc = wview(w_pc[l]); wo_ = wview(w_o[l])
                    for ti in range(NT):
                        tok = slice(ti * T, (ti + 1) * T)
                        for up in range(2):
                            us, bus = W.get(win[:, :, up * 256:(up + 1) * 256], 8, 256)
                            for uu in range(2):
                                c = up * 2 + uu
                                b = bank()
                                for kc in range(8):
                                    mm(ps[:, b, :], us[:, kc, uu * 128:(uu + 1) * 128], hT[:, kc, tok], kc == 0, kc == 7,
                                       [bus, hbuf[ti]], [pbuf[b]], signal=(kc == 7))
                                P.op("act", lambda e, c=c, b=b: e.activation(uT[:, c, :], ps[:, b, :], AF.Gelu_apprx_tanh),
                                     reads=[pbuf[b]], writes=[buT[c]])
                                release(b)
                        vs = W.get_many([(win[:, :, 512 + hf * 256: 512 + (hf + 1) * 256], 8, 256) for hf in range(2)])
                        for blk in range(4):
                            tb = slice(ti * T + blk * 128, ti * T + (blk + 1) * 128)
                            b = bank()
                            for hf in range(2):
                                for kc in range(8):
                                    mm(ps[:, b, hf * 256:(hf + 1) * 256], hT[:, kc, tb], vs[hf][0][:, kc, :], kc == 0, kc == 7,
                                       [vs[hf][1], hbuf[ti]], [pbuf[b]], signal=(kc == 7))
                            P.op("act", lambda e, b=b: e.activation(vtmp, ps[:, b, :], AF.Gelu_apprx_tanh), reads=[pbuf[b]], writes=[bvt])
                            release(b)
                            P.op("dve", lambda e: e.bn_stats(st6[:, 0:6], vtmp), reads=[bvt], writes=[bst6])
                            P.op("dve", lambda e: e.bn_aggr(mv[:, 0:2], st6[:, 0:6]), reads=[bst6], writes=[bmv])
                            P.op("act", lambda e: e.activation(mv[:, 2:3], mv[:, 1:2], AF.Sqrt, bias=eps_ap), reads=[bmv, bcf], writes=[bmv])
                            P.op("dve", lambda e: e.reciprocal(mv[:, 2:3], mv[:, 2:3]), reads=[bmv], writes=[bmv])
                            P.op("dve", lambda e: e.tensor_scalar(vtmp, vtmp, mv[:, 0:1], mv[:, 2:3], ALU.subtract, ALU.mult),
                                 reads=[bmv, bvt], writes=[bvt])
                            P.op("pool", lambda e: e.tensor_tensor(vtmp, vtmp, lng, ALU.mult), reads=[bvt, blnrep], writes=[bvt])
                            P.op("pool", lambda e: e.tensor_tensor(vln, vtmp, lnb, ALU.add), reads=[bvt, blnrep], writes=[bvln])
                            b = bank()
                            for g in range(4):
                                mm(ps[:, b, g * 128:(g + 1) * 128], vln[:, g * 128:(g + 1) * 128], wsT[:, g * 128:(g + 1) * 128], True, True,
                                   [bvln, bwsT], [pbuf[b]], signal=(g == 3))
                            P.op("dve", lambda e, b=b: e.tensor_tensor(svt, ps[:, b, :], bsrep, ALU.add), reads=[pbuf[b], bbsrep], writes=[bsvt])
                            release(b)
                            P.op("dve", lambda e, blk=blk: e.tensor_tensor(yaT[:, :, blk * 128:(blk + 1) * 128],
                                                                            svt.rearrange("p (g t) -> p g t", g=4),
                                                                            uT[:, :, blk * 128:(blk + 1) * 128], ALU.mult),
                                 reads=[bsvt] + buT, writes=[byaT])
                        for m in range(8):
                            mc = slice(m * 128, (m + 1) * 128)
                            gb_ = []
                            for br in range(3):
                                gsl, bgsl = W.get(win[:, :, GATE0 + br * 1024 + m * 128: GATE0 + br * 1024 + (m + 1) * 128], 8, 128)
                                b = bank()
                                for kc in range(8):
                                    mm(ps[:, b, :], gsl[:, kc, :], hT[:, kc, tok], kc == 0, kc == 7, [bgsl, hbuf[ti]], [pbuf[b]], signal=(kc == 7))
                                P.op("act", lambda e, br=br, b=b: e.activation(gsb[br], ps[:, b, :], AF.Sigmoid), reads=[pbuf[b]], writes=[bgsb[br]])
                                release(b)
                            srcs = [(wpa, 4, yaT, lambda kc: [byaT]), (wpb, 2, yb, lambda kc: [byb[kc]]), (wpc, 4, yc, lambda kc: [byc[kc][ti]])]
                            for br, (wv, nk, ysrc, bf_) in enumerate(srcs):
                                psl, bpsl = W.get(wv[:, :, mc], nk, 128)
                                b = bank()
                                for kc in range(nk):
                                    rhs = ysrc[:, kc, :] if br == 0 else ysrc[:, kc, tok]
                                    mm(ps[:, b, :], psl[:, kc, :], rhs, kc == 0, kc == nk - 1, [bpsl] + bf_(kc), [pbuf[b]], signal=(kc == nk - 1))
                                P.op("dve", lambda e, br=br, b=b: e.tensor_tensor(gsb[br], gsb[br], ps[:, b, :], ALU.mult),
                                     reads=[pbuf[b], bgsb[br]], writes=[bgsb[br]])
                                release(b)
                            P.op("pool", lambda e: e.tensor_tensor(gsb[0], gsb[0], gsb[1], ALU.add), reads=[bgsb[0], bgsb[1]], writes=[bgsb[0]])
                            P.op("pool", lambda e, m=m: e.tensor_tensor(merged[:, m, :], gsb[0], gsb[2], ALU.add), reads=[bgsb[0], bgsb[2]], writes=[bmer[m]])
                        bss = bank()
                        for m in range(8):
                            osl, bosl = W.get(wo_[:, :, m * 128:(m + 1) * 128], 8, 128)
                            b = bank()
                            for kc in range(8):
                                mm(ps[:, b, :], osl[:, kc, :], merged[:, kc, :], kc == 0, kc == 7, [bosl, bmer[kc]], [pbuf[b]], signal=(kc == 7))
                            P.op("act", lambda e, m=m, b=b: e.activation(obuf[:, m, :], ps[:, b, :], AF.Copy), reads=[pbuf[b]], writes=[bob[m]])
                            k = m % 2
                            P.op("act", lambda e, k=k, b=b: e.activation(sqe[k], ps[:, b, :], AF.Square), reads=[pbuf[b]], writes=[bsqe[k]])
                            release(b)
                            if m > 0:
                                mm(ps[:, bss, :], ones, sqe[(m - 1) % 2], m == 1, False, [bsqe[(m - 1) % 2], bcbf], [pbuf[bss]])
                        mm(ps[:, bss, :], ones, sqe[7 % 2], False, True, [bsqe[7 % 2], bcbf], [pbuf[bss]])
                        postnorm_residual(l, G_MPOST, ti, obuf, bob, bss, rstd, brstd, tmp, btmp, False)
                    scr_users = keep + mineE

            plan_ = plan
            if plan_ is None:
                plan_ = []
                for l in range(DEPTH):
                    plan_ += [("ffn1", l), ("mix", l), ("ffn2", l)]
            for name, l in plan_:
                if name == "ffn1":
                    ffn(l, 0)
                elif name == "ffn2":
                    ffn(l, 1)
                else:
                    mixer(l)
                if stop == "%s_%d" % (name, l):
                    break
            outv = out.rearrange("(c p) t -> p c t", p=128)
            if dump is not None:
                src, bufs, nch = dump_src[dump]
                P.dma("pool", outv[:, 0:nch, :], src, reads=bufs, dbuf=bufs[0])
                if not P.dry:
                    P._need("sp", (bufs[0].dsem, bufs[0].dcnt))
                return
            for ti in range(NT):
                P.dma("sp", outv[:, :, ti * T:(ti + 1) * T], xT[:, :, ti * T:(ti + 1) * T],
                      reads=[xbuf[c][ti] for c in range(8)], dbuf=xbuf[0][ti])
            for ti in range(NT):
                P._need("sp", (xbuf[0][ti].dsem, xbuf[0][ti].dcnt)) if not P.dry else None

        P.dry = True
        body()
        P.dry = False
        body()
        assert W.i == len(W.plan)
        P.emit()
        print("instr counts:", {e: len(P.q[e]) for e in ENGINES}, "waits", P.nwait, "dma sems", P.ndsem, "scratch cap", scr_cap)
    return nc


def _consts():
    i = np.arange(128)
    r = i[:, None]
    c = i[None, :]
    ident = (r == c).astype(np.float32)
    tri = (r >= c).astype(np.float32)
    ones = np.ones((128, 128), np.float32)
    onesL = np.concatenate([np.ones((128, 64)), np.zeros((128, 64))], 1).astype(np.float32)
    onesR = np.concatenate([np.zeros((128, 64)), np.ones((128, 64))], 1).astype(np.float32)
    Md = np.where(r <= c, 0.0, NEG).astype(np.float32)
    Mp = np.where(c <= r, 0.0, NEG).astype(np.float32)
    mrow = np.concatenate([Mp, Mp, Md, Md], 1)
    zmask = np.where(r >= c, NEG, 0.0).astype(np.float32)
    amask = np.where(r >= c, -NEG, 0.0).astype(np.float32)
    cbf = np.concatenate([ident, tri, ones, onesL, onesR, mrow, zmask, amask, np.zeros((128, 128), np.float32)], 1)
    assert cbf.shape == (128, 1536)
    maskA = (c >= r).astype(np.float32)
    cf = np.zeros((128, 132), np.float32)
    cf[:, 0:128] = maskA
    cf[:, 128] = EPS
    cf[:, 129] = 1.0
    return cbf.astype(ml_dtypes.bfloat16), cf


_NC_CACHE = {}


def _get_nc(stop=None):
    if stop not in _NC_CACHE:
        _NC_CACHE[stop] = build_program(stop=stop)
    return _NC_CACHE[stop]


def make_in_maps(inputs, n_cores=8):
    f = lambda a: np.ascontiguousarray(np.asarray(a, dtype=np.float32))
    x = f(inputs["x"])
    gl = []
    for name in ["ffn1_pre_g", "ffn1_post_g", "mix_pre_g", "mix_post_g", "ffn2_pre_g", "ffn2_post_g"]:
        g = f(inputs[name])
        gl.append(g.reshape(DEPTH, 8, 128).transpose(2, 0, 1).reshape(128, 16))
    gains = np.ascontiguousarray(np.concatenate(gl, 1))
    cbf, cf = _consts()
    ln_g = f(inputs["a_ln_g"]); ln_b = f(inputs["a_ln_b"])
    lnrep = np.ascontiguousarray(np.concatenate([
        np.broadcast_to(ln_g[:, None, :], (DEPTH, 128, 512)),
        np.broadcast_to(ln_b[:, None, :], (DEPTH, 128, 512))], 2))
    bs = f(inputs["a_bs"])
    bsrep = np.ascontiguousarray(np.broadcast_to(bs.reshape(DEPTH, 1, 512), (DEPTH, 128, 512)))
    ws = f(inputs["a_ws"])
    wsT = np.ascontiguousarray(ws.transpose(0, 3, 1, 2).reshape(DEPTH, 128, 512))
    shared = {
        "gains": gains, "cbf": cbf, "cf32": cf, "lnrep": lnrep, "bsrep": bsrep, "wsT": wsT,
        "ffn1_wi": f(inputs["ffn1_wi"]), "ffn2_wi": f(inputs["ffn2_wi"]),
        "ffn1_wo": f(inputs["ffn1_wo"]), "ffn2_wo": f(inputs["ffn2_wo"]),
        "w_in": f(inputs["w_in"]), "w_pa": f(inputs["w_pa"]), "w_pb": f(inputs["w_pb"]),
        "w_pc": f(inputs["w_pc"]), "w_o": f(inputs["w_o"]),
    }
    maps = []
    for b in range(n_cores):
        m = dict(shared)
        m["xT"] = np.ascontiguousarray(x[b].T)
        maps.append(m)
    return maps


def kernel(**inputs):
    nc = _get_nc()
    in_maps = make_in_maps(inputs, 8)
    res = run_bass_kernel_spmd(nc, in_maps, core_ids=list(range(8)))
    outs = [np.asarray(r["outT"]).T for r in res.results]
    return np.ascontiguousarray(np.stack(outs, 0).astype(np.float32))
```

```python
import numpy as np
import ml_dtypes
from contextlib import ExitStack
import concourse.bass as bass
import concourse.mybir as mybir
from concourse.bass_utils import run_bass_kernel_spmd

F32 = mybir.dt.float32
BF16 = mybir.dt.bfloat16
AF = mybir.ActivationFunctionType
ALU = mybir.AluOpType

ENGINES = ("pe", "act", "dve", "pool", "sp")

D = 1024
S = 2048
T = 512
NT = S // T
DFF = 2816
NFC = DFF // 128
DEPTH = 2
QB0, KB0, VB0 = 1024, 1024 + 768, 1024 + 1536
QC0 = 1024 + 2304
KC0, VC0 = QC0 + 512, QC0 + 1024
GATE0 = QC0 + 1536
EPS = 1e-6
NEG = -30000.0
import os
BLEVEL = int(os.environ.get("BLEVEL", "9"))

G_F1PRE, G_F1POST, G_MPRE, G_MPOST, G_F2PRE, G_F2POST = range(6)


class Buf:
    __slots__ = ("name", "w", "r", "dsem", "dcnt")

    def __init__(self, name=""):
        self.name = name
        self.w = None
        self.r = {}
        self.dsem = None
        self.dcnt = 0

    def inherit(self, *olds):
        for o in olds:
            evs = list(o.r.items())
            if o.w is not None:
                evs.append(o.w)
            for k, v in evs:
                if self.r.get(k, 0) < v:
                    self.r[k] = v
        return self


class Prog:
    def __init__(self, nc, stack):
        self.nc = nc
        self.stack = stack
        self.dry = False
        self.q = {e: [] for e in ENGINES}
        self.sem = {}
        self.cnt = {}
        for e in ENGINES:
            self.sem[e] = stack.enter_context(nc.semaphore("s_" + e))
            self.cnt[e] = 0
        self.seen = {e: {} for e in ENGINES}
        self.ndsem = 0
        self.pending = {e: False for e in ENGINES}
        self.nwait = 0

    def _need(self, eng, dep):
        if dep is None:
            return
        key, val = dep
        if key == "pe" and eng == "pe":
            return
        if self.seen[eng].get(key, 0) >= val:
            return
        if key in self.cnt:
            assert val <= self.cnt[key], ("wait on future signal", eng, key, val, self.cnt[key])
        self.seen[eng][key] = val
        sem = self.sem[key]
        self.nwait += 1
        self.q[eng].append(lambda e, sem=sem, val=val: e.wait_ge(sem, val))

    def _deps(self, eng, reads, writes):
        for b in reads:
            self._need(eng, b.w)
        for b in writes:
            self._need(eng, b.w)
            for k, v in list(b.r.items()):
                self._need(eng, (k, v))

    def op(self, eng, fn, reads=(), writes=(), signal=True):
        if self.dry:
            return
        if eng != "pe":
            excl = [b for b in reads if b.name.startswith("bank") and b not in writes]
            if excl:
                writes = list(writes) + excl
        self._deps(eng, reads, writes)
        val = self.cnt[eng] + 1
        ev = (eng, val)
        for b in reads:
            b.r[eng] = val
        for b in writes:
            b.w = ev
            b.r = {}
        if signal:
            self.cnt[eng] = val
            self.pending[eng] = False
            sem = self.sem[eng]
            self.q[eng].append(lambda e, fn=fn, sem=sem: fn(e).then_inc(sem, 1))
        else:
            assert eng == "pe"
            self.pending[eng] = True
            self.q[eng].append(lambda e, fn=fn: fn(e))

    def dma(self, eng, out, in_, reads=(), writes=(), dbuf=None):
        if self.dry:
            return
        self._deps(eng, reads, writes)
        if dbuf is None:
            dbuf = writes[0] if writes else reads[0]
        if dbuf.dsem is None:
            key = "d%d" % self.ndsem
            self.ndsem += 1
            self.sem[key] = self.stack.enter_context(self.nc.semaphore("s_" + key))
            dbuf.dsem = key
        dbuf.dcnt += 16
        ev = (dbuf.dsem, dbuf.dcnt)
        for b in reads:
            b.r[dbuf.dsem] = dbuf.dcnt
        for b in writes:
            b.w = ev
            b.r = {}
        sem = self.sem[dbuf.dsem]
        self.q[eng].append(lambda e, out=out, in_=in_, sem=sem: e.dma_start(out=out, in_=in_).then_inc(sem, 16))

    def barrier(self):
        if self.dry:
            return
        engs = ("pe", "act", "dve", "pool")
        for e in engs:
            for o in engs:
                if o != e and self.cnt[o] > 0:
                    self._need(e, (o, self.cnt[o]))

    def emit(self):
        nc = self.nc
        assert not any(self.pending.values()), self.pending
        with nc.Block() as block:
            @block.tensor
            def _(e):
                for f in self.q["pe"]:
                    f(e)

            @block.scalar
            def _(e):
                for f in self.q["act"]:
                    f(e)

            @block.vector
            def _(e):
                for f in self.q["dve"]:
                    f(e)

            @block.gpsimd
            def _(e):
                for f in self.q["pool"]:
                    f(e)

            @block.sync
            def _(e):
                for f in self.q["sp"]:
                    f(e)


class SubArena:
    def __init__(self, t, base, cap):
        self.t = t
        self.base = base
        self.cap = cap
        self.off = 0

    def reset(self):
        self.off = 0

    def _alloc(self, nbytes):
        off = (self.off + 63) // 64 * 64
        assert off + nbytes <= self.cap, ("arena overflow", off, nbytes, self.cap)
        self.off = off + nbytes
        return self.base + off

    def bf(self, nelem):
        o = self._alloc(nelem * 2)
        return self.t[:, o // 2: o // 2 + nelem]

    def f32(self, nelem):
        o = self._alloc(nelem * 4)
        return self.t[:, o // 2: o // 2 + 2 * nelem].bitcast(F32)


class WStream:
    def __init__(self, P, classes):
        self.P = P
        self.cls = classes
        self.plan = []
        self.i = 0
        self.next_emit = 0
        self.done = {c: 0 for c in classes}
        self.num = {c: 0 for c in classes}

    def _cls_for(self, nelem):
        best = None
        for c, slots in self.cls.items():
            cap = slots[0][0].shape[1]
            if nelem <= cap and (best is None or cap < self.cls[best][0][0].shape[1]):
                best = c
        assert best is not None, nelem
        return best

    def get(self, src, nk, ncols):
        return self.get_many([(src, nk, ncols)])[0]

    def get_many(self, specs):
        P = self.P
        outs = []
        if P.dry:
            for src, nk, ncols in specs:
                nelem = nk * ncols
                c = self._cls_for(nelem)
                assert len(specs) <= len(self.cls[c])
                self.plan.append((src, nk, ncols, c, self.num[c]))
                self.num[c] += 1
                view, buf = self.cls[c][0]
                outs.append((view[:, 0:nelem].rearrange("p (k n) -> p k n", k=nk), Buf()))
            return outs
        last = self.i + len(specs) - 1
        while self.next_emit < len(self.plan):
            s2, nk2, nc2, c2, k2 = self.plan[self.next_emit]
            n2 = len(self.cls[c2])
            if k2 - self.done[c2] > n2 - 1:
                break
            view2, buf2 = self.cls[c2][k2 % n2]
            dst = view2[:, 0:nk2 * nc2].rearrange("p (k n) -> p k n", k=nk2)
            P.dma("pool", dst, s2, writes=[buf2])
            self.next_emit += 1
        assert self.next_emit > last, ("slab not emitted", self.i)
        for src, nk, ncols in specs:
            src_, nk_, ncols_, c, k = self.plan[self.i]
            assert (nk_, ncols_) == (nk, ncols), ("wstream mismatch", self.i)
            n = len(self.cls[c])
            view, buf = self.cls[c][k % n]
            outs.append((view[:, 0:nk * ncols].rearrange("p (k n) -> p k n", k=nk), buf))
            self.i += 1
        for src, nk, ncols in specs:
            c = self._cls_for(nk * ncols)
            self.done[c] += 1
        return outs


class LazyDram:
    def __init__(self, nc, name, shape, dtype=F32):
        self.nc, self.name, self.shape, self.dtype, self._ap = nc, name, shape, dtype, None

    def __getitem__(self, idx):
        if self._ap is None:
            self._ap = self.nc.dram_tensor(self.name, list(self.shape), self.dtype, kind="ExternalInput").ap()
            DECLARED.append(self.name)
        return self._ap[idx]


DECLARED = []


def build_program(stop=None, plan=None, parts="BCE", dump=None):
    nc = bass.Bass("TRN2", target_bir_lowering=False)
    del DECLARED[:]

    def dram(n, s, d=F32, kind="ExternalInput"):
        if kind == "ExternalInput":
            DECLARED.append(n)
        return nc.dram_tensor(n, list(s), d, kind=kind).ap()
    xin = dram("xT", [D, S])
    out = dram("outT", [D, S], kind="ExternalOutput")
    gains_d = dram("gains", [128, 96])
    cbf_d = dram("cbf", [128, 1536], BF16)
    cf_d = dram("cf32", [128, 132])
    lnrep_d = dram("lnrep", [DEPTH, 128, 1024])
    bsrep_d = dram("bsrep", [DEPTH, 128, 512])
    wsT_d = dram("wsT", [DEPTH, 128, 512])
    ffn_wi = [LazyDram(nc, "ffn1_wi", [DEPTH, D, 2 * DFF]), LazyDram(nc, "ffn2_wi", [DEPTH, D, 2 * DFF])]
    ffn_wo = [LazyDram(nc, "ffn1_wo", [DEPTH, DFF, D]), LazyDram(nc, "ffn2_wo", [DEPTH, DFF, D])]
    w_in = LazyDram(nc, "w_in", [DEPTH, D, 7936])
    w_pa = LazyDram(nc, "w_pa", [DEPTH, 512, D])
    w_pb = LazyDram(nc, "w_pb", [DEPTH, 256, D])
    w_pc = LazyDram(nc, "w_pc", [DEPTH, 512, D])
    w_o = LazyDram(nc, "w_o", [DEPTH, D, D])

    def wview(w2d):
        return w2d.rearrange("(kc p) n -> p kc n", p=128)

    with ExitStack() as st:
        P = Prog(nc, st)
        total = int(nc.sbuf_bytes_remaining) // 64 * 64 - 256
        arena_t = st.enter_context(nc.sbuf_tensor("arena", [128, total // 2], BF16))
        main = SubArena(arena_t, 0, total)
        ps = st.enter_context(nc.psum_tensor("ps", [128, 8, 512], F32))
        pbuf = [Buf("bank%d" % i) for i in range(8)]
        free_banks = list(range(8))

        def bank():
            return free_banks.pop(0)

        def release(b):
            free_banks.append(b)

        xT = main.f32(8 * S).rearrange("p (c t) -> p c t", c=8)
        xbuf = [[Buf("x%d_%d" % (c, ti)) for ti in range(NT)] for c in range(8)]
        hT = main.bf(8 * S).rearrange("p (c t) -> p c t", c=8)
        hbuf = [Buf("h%d" % ti) for ti in range(NT)]
        gains = main.f32(96); bgains = Buf("gains")
        ghalf = main.f32(96); bghalf = Buf("ghalf")
        cbf = main.bf(1536); bcbf = Buf("cbf")
        cf = main.f32(132); bcf = Buf("cf")
        lnrep = main.f32(1024); blnrep = Buf("lnrep")
        bsrep = main.f32(512); bbsrep = Buf("bsrep")
        wsT = main.bf(512); bwsT = Buf("wsT")
        NSMALL = 7
        small = [(main.bf(2048), Buf("ws%d" % i)) for i in range(NSMALL)]
        W = WStream(P, {"s": small})
        scr_cap = total - ((main.off + 63) // 64 * 64)
        scr = SubArena(arena_t, (main.off + 63) // 64 * 64, scr_cap)
        scr_users = []

        ident = cbf[:, 0:128]
        tri = cbf[:, 128:256]
        ones = cbf[:, 256:384]
        onesL = cbf[:, 384:512]
        onesR = cbf[:, 512:640]
        mrow = cbf[:, 640:1152]
        zmask = cbf[:, 1152:1280]
        amask = cbf[:, 1280:1408]
        maskA = cf[:, 0:128]
        eps_ap = cf[:, 128:129]
        one_ap = cf[:, 129:130]

        def gcol(gi, l, c):
            o = gi * 16 + l * 8 + c
            return gains[:, o:o + 1]

        def ghcol(gi, l, c):
            o = gi * 16 + l * 8 + c
            return ghalf[:, o:o + 1]

        def mm(out_, lhsT, rhs, start, stop, reads, writes, signal=True):
            P.op("pe", lambda e: e.matmul(out_, lhsT, rhs, start=start, stop=stop), reads=reads, writes=writes, signal=signal)

        dump_src = {}

        def new_scratch(bufs):
            for b in bufs:
                b.inherit(*scr_users)
            return bufs

        def body():
            nonlocal scr_users
            free_banks[:] = list(range(8))
            P.dma("sp", gains, gains_d, writes=[bgains])
            P.dma("sp", cbf, cbf_d, writes=[bcbf])
            P.dma("sp", cf, cf_d, writes=[bcf])
            for ti in range(NT):
                P.dma("sp", xT[:, :, ti * T:(ti + 1) * T], xin.rearrange("(c p) t -> p c t", p=128)[:, :, ti * T:(ti + 1) * T],
                      writes=[xbuf[c][ti] for c in range(8)], dbuf=xbuf[0][ti])
            P.op("dve", lambda e: e.tensor_scalar(ghalf, gains, 0.5, None, ALU.mult), reads=[bgains], writes=[bghalf])

            def rstd_from_bank(bss, dst, dstbuf):
                P.op("act", lambda e: e.activation(dst, ps[:, bss, :], AF.Sqrt, bias=eps_ap, scale=1.0 / D),
                     reads=[pbuf[bss], bcf], writes=[dstbuf])
                P.op("dve", lambda e: e.reciprocal(dst, dst), reads=[dstbuf], writes=[dstbuf])

            def prenorm(l, gi, ti, sq, bsq, rstd, brstd):
                tok = slice(ti * T, (ti + 1) * T)
                bss = bank()
                for c in range(8):
                    k = c % 2
                    P.op("act", lambda e, c=c, k=k: e.activation(sq[k], xT[:, c, tok], AF.Square),
                         reads=[xbuf[c][ti]], writes=[bsq[k]])
                    mm(ps[:, bss, :], ones, sq[k], c == 0, c == 7, [bsq[k], bcbf], [pbuf[bss]])
                rstd_from_bank(bss, rstd, brstd)
                release(bss)
                for c in range(8):
                    P.op("dve", lambda e, c=c: e.scalar_tensor_tensor(hT[:, c, tok], xT[:, c, tok], gcol(gi, l, c), rstd, ALU.mult, ALU.mult),
                         reads=[xbuf[c][ti], brstd, bgains], writes=[hbuf[ti]])

            def prenorm_a(ti, sq8, bsq8):
                tok = slice(ti * T, (ti + 1) * T)
                for c in range(8):
                    P.op("act", lambda e, c=c: e.activation(sq8[c], xT[:, c, tok], AF.Square),
                         reads=[xbuf[c][ti]], writes=[bsq8[c]])

            def prenorm_b(l, gi, ti, sq8, bsq8, rstd, brstd):
                tok = slice(ti * T, (ti + 1) * T)
                bss = bank()
                for c in range(8):
                    mm(ps[:, bss, :], ones, sq8[c], c == 0, c == 7, [bsq8[c], bcbf], [pbuf[bss]], signal=(c == 7))
                rstd_from_bank(bss, rstd, brstd)
                release(bss)
                for c in range(8):
                    P.op("dve", lambda e, c=c: e.scalar_tensor_tensor(hT[:, c, tok], xT[:, c, tok], gcol(gi, l, c), rstd, ALU.mult, ALU.mult),
                         reads=[xbuf[c][ti], brstd, bgains], writes=[hbuf[ti]])

            def postnorm_residual(l, gi, ti, ybuf, bybuf, bss, rstd, brstd, tmp, btmp, half):
                tok = slice(ti * T, (ti + 1) * T)
                rstd_from_bank(bss, rstd, brstd)
                release(bss)
                for c in range(8):
                    k = c % 2
                    sc = ghcol(gi, l, c) if half else gcol(gi, l, c)
                    P.op("dve", lambda e, c=c, k=k, sc=sc: e.scalar_tensor_tensor(tmp[k], ybuf[:, c, :], sc, rstd, ALU.mult, ALU.mult),
                         reads=[bybuf[c], brstd, bghalf, bgains], writes=[btmp[k]])
                    P.op("pool", lambda e, c=c, k=k: e.tensor_tensor(xT[:, c, tok], xT[:, c, tok], tmp[k], ALU.add),
                         reads=[btmp[k], xbuf[c][ti]], writes=[xbuf[c][ti]])

            def ffn(l, which):
                nonlocal scr_users
                scr.reset()
                gi_pre = G_F1PRE if which == 0 else G_F2PRE
                gi_post = G_F1POST if which == 0 else G_F2POST
                wi = wview(ffn_wi[which][l])
                wo = wview(ffn_wo[which][l])
                act = scr.bf(NFC * T).rearrange("p (c t) -> p c t", c=NFC)
                bact = [Buf("act%d" % j) for j in range(NFC)]
                ybuf = scr.f32(8 * T).rearrange("p (c t) -> p c t", c=8)
                bybuf = [Buf("y%d" % c) for c in range(8)]
                sq = [scr.bf(T) for _ in range(2)]; bsq = [Buf("sq%d" % i) for i in range(2)]
                stmp = [scr.f32(T) for _ in range(2)]; bstmp = [Buf("st%d" % i) for i in range(2)]
                rstd = scr.f32(T); brstd = Buf("rstd")
                tmp = [scr.f32(T) for _ in range(2)]; btmp = [Buf("tmp%d" % i) for i in range(2)]
                sq8 = [scr.bf(T) for _ in range(8)]; bsq8 = [Buf("sq8_%d" % i) for i in range(8)]
                rstdp = scr.f32(T); brstdp = Buf("rstdp")
                mine = bact + bybuf + bsq + bstmp + [brstd] + btmp + bsq8 + [brstdp]
                new_scratch(mine)
                prenorm_a(0, sq8, bsq8)
                prenorm_b(l, gi_pre, 0, sq8, bsq8, rstdp, brstdp)
                for ti in range(NT):
                    tok = slice(ti * T, (ti + 1) * T)
                    for jp in range(NFC // 2):
                        if ti + 1 < NT and jp == 4:
                            prenorm_a(ti + 1, sq8, bsq8)
                        if ti + 1 < NT and jp == 8:
                            prenorm_b(l, gi_pre, ti + 1, sq8, bsq8, rstdp, brstdp)
                        (gs, bgs), (us, bus) = W.get_many([(wi[:, :, jp * 256:(jp + 1) * 256], 8, 256),
                                                           (wi[:, :, DFF + jp * 256:DFF + (jp + 1) * 256], 8, 256)])
                        for jj in range(2):
                            j = jp * 2 + jj
                            bg = bank(); bu = bank()
                            for kc in range(8):
                                mm(ps[:, bg, :], gs[:, kc, jj * 128:(jj + 1) * 128], hT[:, kc, tok], kc == 0, kc == 7,
                                   [bgs, hbuf[ti]], [pbuf[bg]], signal=(kc == 7))
                            for kc in range(8):
                                mm(ps[:, bu, :], us[:, kc, jj * 128:(jj + 1) * 128], hT[:, kc, tok], kc == 0, kc == 7,
                                   [bus, hbuf[ti]], [pbuf[bu]], signal=(kc == 7))
                            k = j % 2
                            P.op("act", lambda e, k=k, bg=bg: e.activation(stmp[k], ps[:, bg, :], AF.Silu),
                                 reads=[pbuf[bg]], writes=[bstmp[k]])
                            P.op("dve", lambda e, k=k, bu=bu, j=j: e.tensor_tensor(act[:, j, :], stmp[k], ps[:, bu, :], ALU.mult),
                                 reads=[bstmp[k], pbuf[bu]], writes=[bact[j]])
                            release(bg); release(bu)
                    bss = bank()
                    for m in range(8):
                        H2 = NFC // 2
                        (w0_, bw0), (w1_, bw1) = W.get_many([(wo[:, 0:H2, m * 128:(m + 1) * 128], H2, 128),
                                                             (wo[:, H2:NFC, m * 128:(m + 1) * 128], H2, 128)])
                        b = bank()
                        for fc in range(NFC):
                            ws_, bws = (w0_, bw0) if fc < H2 else (w1_, bw1)
                            mm(ps[:, b, :], ws_[:, fc % H2, :], act[:, fc, :], fc == 0, fc == NFC - 1,
                               [bws, bact[fc]], [pbuf[b]], signal=(fc == NFC - 1))
                        P.op("act", lambda e, m=m, b=b: e.activation(ybuf[:, m, :], ps[:, b, :], AF.Copy),
                             reads=[pbuf[b]], writes=[bybuf[m]])
                        k = m % 2
                        P.op("act", lambda e, k=k, b=b: e.activation(sq[k], ps[:, b, :], AF.Square),
                             reads=[pbuf[b]], writes=[bsq[k]])
                        release(b)
                        if m > 0:
                            mm(ps[:, bss, :], ones, sq[(m - 1) % 2], m == 1, False, [bsq[(m - 1) % 2], bcbf], [pbuf[bss]])
                    mm(ps[:, bss, :], ones, sq[7 % 2], False, True, [bsq[7 % 2], bcbf], [pbuf[bss]])
                    postnorm_residual(l, gi_post, ti, ybuf, bybuf, bss, rstd, brstd, tmp, btmp, True)
                scr_users = mine

            def mixer(l):
                nonlocal scr_users
                scr.reset()
                win = wview(w_in[l])
                P.dma("sp", lnrep, lnrep_d[l], writes=[blnrep])
                P.dma("sp", bsrep, bsrep_d[l], writes=[bbsrep])
                yb = scr.bf(2 * S).rearrange("p (c t) -> p c t", c=2)
                byb = [Buf("yb%d" % j) for j in range(2)]
                yc = scr.bf(4 * S).rearrange("p (c t) -> p c t", c=4)
                byc = [[Buf("yc%d_%d" % (j, ti)) for ti in range(NT)] for j in range(4)]
                keep = byb + [b for row in byc for b in row]
                new_scratch(keep)
                dump_src["yb"] = (yb, byb, 2)
                dump_src["yc"] = (yc, [b for row in byc for b in row], 4)
                scr_mark = scr.off
                sq = [scr.bf(T) for _ in range(2)]; bsq = [Buf() for _ in range(2)]
                rstd = scr.f32(T); brstd = Buf()
                wtmp = scr.f32(512); bwtmp = Buf()
                pre = new_scratch(bsq + [brstd, bwtmp])
                P.dma("sp", wtmp, wsT_d[l], writes=[bwtmp])
                for g in range(4):
                    P.op("dve", lambda e, g=g: e.tensor_tensor(wsT[:, g * 128:(g + 1) * 128], wtmp[:, g * 128:(g + 1) * 128], maskA, ALU.mult),
                         reads=[bwtmp, bcf], writes=[bwsT])
                for ti in range(NT):
                    prenorm(l, G_MPRE, ti, sq, bsq, rstd, brstd)
                scr_users = keep + pre

                scr.off = scr_mark
                P.barrier()
                if "B" in parts:
                    qpz = [scr.bf(S) for _ in range(2)]; kperm = scr.bf(S)
                    bqp = Buf("qperm"); bkp = Buf("kperm")
                    vz = scr.bf(16 * 2 * 128).rearrange("p (b h d) -> p b h d", b=16, h=2)
                    bvz = [Buf("vz%d" % i) for i in range(4)]
                    numacc = scr.f32(S); denacc = scr.f32(S)
                    bnum = Buf("num"); bden = Buf("den")
                    pT = [scr.bf(512) for _ in range(3)]; bpT = [Buf() for _ in range(3)]
                    mineB = new_scratch([bqp, bkp] + bvz + [bnum, bden] + bpT)
                    for i4 in range(4):
                        P.op("pool", lambda e, i4=i4: e.memset(vz[:, i4 * 4:(i4 + 1) * 4], 0.0), writes=[bvz[i4]])
                    for hh in range(2):
                        P.op("pool", lambda e, hh=hh: e.memset(qpz[hh], 0.0), writes=[bqp])
                    pcount = 0
                    for j in range(2):
                        for g in range(3):
                            r = (1, 4, 16)[g]
                            Lc = S // r
                            nblk = Lc // 128
                            (sq_, bsq_), (sk_, bsk_), (sv_, bsv_) = W.get_many([
                                (win[:, :, QB0 + g * 256 + j * 128: QB0 + g * 256 + (j + 1) * 128], 8, 128),
                                (win[:, :, KB0 + g * 256 + j * 128: KB0 + g * 256 + (j + 1) * 128], 8, 128),
                                (win[:, :, VB0 + g * 256 + j * 128: VB0 + g * 256 + (j + 1) * 128], 8, 128)])
                            qp3 = [qpz[hh].rearrange("p (c i) -> p c i", c=r) for hh in range(2)]
                            kp3 = kperm.rearrange("p (c i) -> p c i", c=r)
                            w = T // r
                            for ti in range(NT):
                                tok = slice(ti * T, (ti + 1) * T)
                                bq = bank(); bk = bank()
                                for kc in range(8):
                                    mm(ps[:, bq, :], sq_[:, kc, :], hT[:, kc, tok], kc == 0, kc == 7, [bsq_, hbuf[ti]], [pbuf[bq]], signal=(kc == 7))
                                for kc in range(8):
                                    mm(ps[:, bk, :], sk_[:, kc, :], hT[:, kc, tok], kc == 0, kc == 7, [bsk_, hbuf[ti]], [pbuf[bk]], signal=(kc == 7))
                                for hh in range(2):
                                    hs = slice(hh * 64, (hh + 1) * 64)
                                    P.op("act", lambda e, bq=bq, ti=ti, qp3=qp3, r=r, w=w, hh=hh, hs=hs: e.activation(
                                        qp3[hh][hs, :, ti * w:(ti + 1) * w], ps[hs, bq, :].rearrange("p (i c) -> p c i", c=r), AF.Copy, scale=0.125),
                                        reads=[pbuf[bq]], writes=[bqp])
                                P.op("dve", lambda e, bk=bk, ti=ti, kp3=kp3, r=r, w=w: e.tensor_copy(
                                    kp3[:, :, ti * w:(ti + 1) * w], ps[:, bk, :].rearrange("p (i c) -> p c i", c=r)),
                                    reads=[pbuf[bk]], writes=[bkp])
                                release(bq); release(bk)
                            for b4 in range(4):
                                bv = bank()
                                for bb in range(4):
                                    bp = b4 * 4 + bb
                                    p0 = bp * 128
                                    c = p0 // Lc
                                    i0 = p0 % Lc
                                    t0 = c + r * i0
                                    for kc in range(8):
                                        mm(ps[:, bv, bb * 128:(bb + 1) * 128], hT[:, kc, t0:t0 + r * 127 + 1:r], sv_[:, kc, :],
                                           kc == 0, kc == 7, [bsv_] + hbuf, [pbuf[bv]], signal=(kc == 7))
                                pv = ps[:, bv, :].rearrange("p (b d) -> p b d", b=4)
                                P.op("dve", lambda e, b4=b4, pv=pv: e.tensor_copy(vz[:, b4 * 4:(b4 + 1) * 4, 0, 0:64], pv[:, :, 0:64]),
                                     reads=[pbuf[bv]], writes=[bvz[b4]])
                                P.op("act", lambda e, b4=b4, pv=pv: e.activation(vz[:, b4 * 4:(b4 + 1) * 4, 1, 64:128], pv[:, :, 64:128], AF.Copy),
                                     reads=[pbuf[bv]], writes=[bvz[b4]])
                                release(bv)
                            def b_stage1(c, qb):
                                nonlocal pcount
                                bpq = c * nblk + qb
                                kbs = ([bpq - 1] if qb > 0 else []) + [bpq]
                                nk = len(kbs)
                                Wd = nk * 256
                                bs_ = bank()
                                mm(ps[:, bs_, 0:Wd], ident, mrow[:, 512 - Wd:512], True, False, [bcbf], [pbuf[bs_]], signal=False)
                                for a, kb in enumerate(kbs):
                                    for hh in range(2):
                                        last = (a == nk - 1 and hh == 1)
                                        mm(ps[:, bs_, a * 256 + hh * 128: a * 256 + (hh + 1) * 128],
                                           kperm[:, kb * 128:(kb + 1) * 128],
                                           qpz[hh][:, bpq * 128:(bpq + 1) * 128],
                                           False, last, [bqp, bkp], [pbuf[bs_]], signal=last)
                                k = pcount % 3
                                pcount += 1
                                P.op("act", lambda e, k=k, bs_=bs_, Wd=Wd: e.activation(pT[k][:, 0:Wd], ps[:, bs_, 0:Wd], AF.Exp),
                                     reads=[pbuf[bs_]], writes=[bpT[k]])
                                release(bs_)
                                return (c, qb, kbs, nk, k)

                            def b_stage2(c, qb, kbs, nk, k):
                                bo = bank()
                                n = 0
                                for a, kb in enumerate(kbs):
                                    for hh in range(2):
                                        n += 1
                                        mm(ps[:, bo, 0:128], vz[:, kb, hh, :], pT[k][:, a * 256 + hh * 128: a * 256 + (hh + 1) * 128],
                                           n == 1, n == 2 * nk, [bvz[kb // 4], bpT[k]], [pbuf[bo]], signal=(n == 2 * nk))
                                n = 0
                                for a, kb in enumerate(kbs):
                                    for hh in range(2):
                                        n += 1
                                        mm(ps[:, bo, 128:256], onesL if hh == 0 else onesR, pT[k][:, a * 256 + hh * 128: a * 256 + (hh + 1) * 128],
                                           n == 1, n == 2 * nk, [bcbf, bpT[k]], [pbuf[bo]], signal=(n == 2 * nk))
                                cols = slice(c + r * qb * 128, c + r * qb * 128 + r * 127 + 1, r)
                                if g == 0:
                                    P.op("dve", lambda e, bo=bo, cols=cols: e.tensor_copy(numacc[:, cols], ps[:, bo, 0:128]),
                                         reads=[pbuf[bo]], writes=[bnum])
                                    P.op("act", lambda e, bo=bo, cols=cols: e.activation(denacc[:, cols], ps[:, bo, 128:256], AF.Copy),
                                         reads=[pbuf[bo]], writes=[bden])
                                else:
                                    P.op("dve", lambda e, bo=bo, cols=cols: e.tensor_tensor(numacc[:, cols], numacc[:, cols], ps[:, bo, 0:128], ALU.add),
                                         reads=[pbuf[bo]], writes=[bnum])
                                    P.op("dve", lambda e, bo=bo, cols=cols: e.tensor_tensor(denacc[:, cols], denacc[:, cols], ps[:, bo, 128:256], ALU.add),
                                         reads=[pbuf[bo]], writes=[bden])
                                release(bo)

                            b_items = [(c, qb) for c in range(r) for qb in range(nblk)]
                            prev = None
                            for it in b_items:
                                cur = b_stage1(*it)
                                if prev is not None:
                                    b_stage2(*prev)
                                prev = cur
                            b_stage2(*prev)
                        if BLEVEL < 4:
                            P.op("dve", lambda e: e.memset(denacc, 1.0), writes=[bden])
                            P.op("dve", lambda e: e.memset(numacc, 1.0), writes=[bnum])
                        P.op("dve", lambda e: e.reciprocal(denacc, denacc), reads=[bden], writes=[bden])
                        P.op("dve", lambda e, j=j: e.tensor_tensor(yb[:, j, :], numacc, denacc, ALU.mult), reads=[bnum, bden], writes=[byb[j]])
                    scr_users = keep + mineB

                scr.off = scr_mark
                P.barrier()
                if "C" in parts:
                    qz = [scr.bf(S) for _ in range(2)]; kT = scr.bf(S); nkT = scr.bf(S)
                    bqT = Buf("qT"); bkT = Buf("kT"); bnkT = Buf("nkT")
                    vzc = scr.bf(16 * 2 * 128).rearrange("p (b h d) -> p b h d", b=16, h=2)
                    bvzc = [Buf("cvz%d" % i) for i in range(4)]
                    NR = 3
                    ebuf = [scr.f32(T) for _ in range(NR)]; beb = [Buf() for _ in range(NR)]
                    Lbuf = [scr.bf(T) for _ in range(NR)]; bLb = [Buf() for _ in range(NR)]
                    Abuf = [scr.bf(T) for _ in range(NR)]; bAb = [Buf() for _ in range(NR)]
                    Rb = [scr.bf(T) for _ in range(2)]; bRb = [Buf() for _ in range(2)]
                    mineC = new_scratch([bqT, bkT, bnkT] + bvzc + beb + bLb + bAb + bRb)
                    for i4 in range(4):
                        P.op("pool", lambda e, i4=i4: e.memset(vzc[:, i4 * 4:(i4 + 1) * 4], 0.0), writes=[bvzc[i4]])
                    for hh in range(2):
                        P.op("pool", lambda e, hh=hh: e.memset(qz[hh], 0.0), writes=[bqT])
                    itc = 0
                    for j in range(4):
                        (sq_, bsq_), (sk_, bsk_), (sv_, bsv_) = W.get_many([
                            (win[:, :, QC0 + j * 128: QC0 + (j + 1) * 128], 8, 128),
                            (win[:, :, KC0 + j * 128: KC0 + (j + 1) * 128], 8, 128),
                            (win[:, :, VC0 + j * 128: VC0 + (j + 1) * 128], 8, 128)])
                        for ti in range(NT):
                            tok = slice(ti * T, (ti + 1) * T)
                            bq = bank(); bk = bank()
                            for kc in range(8):
                                mm(ps[:, bq, :], sq_[:, kc, :], hT[:, kc, tok], kc == 0, kc == 7, [bsq_, hbuf[ti]], [pbuf[bq]], signal=(kc == 7))
                            for kc in range(8):
                                mm(ps[:, bk, :], sk_[:, kc, :], hT[:, kc, tok], kc == 0, kc == 7, [bsk_, hbuf[ti]], [pbuf[bk]], signal=(kc == 7))
                            for hh in range(2):
                                hs = slice(hh * 64, (hh + 1) * 64)
                                P.op("act", lambda e, bq=bq, tok=tok, hh=hh, hs=hs: e.activation(qz[hh][hs, tok], ps[hs, bq, :], AF.Copy, scale=0.125),
                                     reads=[pbuf[bq]], writes=[bqT])
                            P.op("dve", lambda e, bk=bk, tok=tok: e.tensor_copy(kT[:, tok], ps[:, bk, :]), reads=[pbuf[bk]], writes=[bkT])
                            P.op("act", lambda e, bk=bk, tok=tok: e.activation(nkT[:, tok], ps[:, bk, :], AF.Copy, scale=-1.0),
                                 reads=[pbuf[bk]], writes=[bnkT])
                            release(bq); release(bk)
                        for b4 in range(4):
                            bv = bank()
                            for bb in range(4):
                                blk = b4 * 4 + bb
                                for kc in range(8):
                                    mm(ps[:, bv, bb * 128:(bb + 1) * 128], hT[:, kc, blk * 128:(blk + 1) * 128], sv_[:, kc, :],
                                       kc == 0, kc == 7, [bsv_, hbuf[blk // 4]], [pbuf[bv]], signal=(kc == 7))
                            pv = ps[:, bv, :].rearrange("p (b d) -> p b d", b=4)
                            P.op("dve", lambda e, b4=b4, pv=pv: e.tensor_copy(vzc[:, b4 * 4:(b4 + 1) * 4, 0, 0:64], pv[:, :, 0:64]),
                                 reads=[pbuf[bv]], writes=[bvzc[b4]])
                            P.op("act", lambda e, b4=b4, pv=pv: e.activation(vzc[:, b4 * 4:(b4 + 1) * 4, 1, 64:128], pv[:, :, 64:128], AF.Copy),
                                 reads=[pbuf[bv]], writes=[bvzc[b4]])
                            release(bv)
                        for ti in range(NT):
                            bo = bank()
                            nsteps = 4 * ti + 4
                            items = [(hh, step) for step in range(nsteps) for hh in range(2)]
                            nit = len(items)
                            state = {}
                            for hh in range(2):
                                P.op("pool", lambda e, hh=hh: e.memset(Rb[hh], 0.0), writes=[bRb[hh]])

                            def info(n):
                                hh, step = items[n]
                                sb = 4 * ti + 3 - step
                                a = sb - 4 * ti
                                col0 = 128 * a if a > 0 else 0
                                return hh, step, sb, a >= 0, col0

                            zstate = {}

                            def S1a(n):
                                hh, step, sb, diag, col0 = info(n)
                                bz = bank()
                                zstate[n] = bz
                                mm(ps[:, bz, col0:T], kT[:, sb * 128:(sb + 1) * 128], qz[hh][:, ti * T + col0:(ti + 1) * T],
                                   True, not diag, [bkT, bqT], [pbuf[bz]], signal=True)
                                if diag:
                                    mm(ps[:, bz, col0:col0 + 128], ident, zmask, False, True, [bcbf], [pbuf[bz]], signal=True)

                            def S1b(n):
                                hh, step, sb, diag, col0 = info(n)
                                k = (itc + n) % NR
                                bz = zstate.pop(n)
                                P.op("act", lambda e: e.activation(ebuf[k][:, col0:T], ps[:, bz, col0:T], AF.Exp),
                                     reads=[pbuf[bz]], writes=[beb[k]])
                                release(bz)

                            def S2(n):
                                hh, step, sb, diag, col0 = info(n)
                                k = (itc + n) % NR
                                hs = slice(hh * 64, (hh + 1) * 64)
                                P.op("act", lambda e: e.activation(Lbuf[k][:, col0:T], ebuf[k][:, col0:T], AF.Ln, bias=one_ap),
                                     reads=[beb[k], bcf], writes=[bLb[k]])
                                ba = bank()
                                state[n] = ba
                                mm(ps[:, ba, col0:T], tri, Lbuf[k][:, col0:T], True, False, [bcbf, bLb[k]], [pbuf[ba]], signal=False)
                                if step > 0:
                                    mm(ps[:, ba, col0:T], ones, Rb[hh][:, col0:T], False, False, [bcbf, bRb[hh]], [pbuf[ba]], signal=False)
                                mm(ps[:, ba, col0:T], nkT[:, sb * 128:(sb + 1) * 128], qz[hh][:, ti * T + col0:(ti + 1) * T],
                                   False, not diag, [bnkT, bqT], [pbuf[ba]], signal=True)
                                if diag:
                                    mm(ps[:, ba, col0:col0 + 128], ident, amask, False, True, [bcbf], [pbuf[ba]], signal=True)
                                if step < nsteps - 1:
                                    P.op("pool", lambda e: e.tensor_tensor(Rb[hh][:, col0:T], Rb[hh][:, col0:T], Lbuf[k][:, col0:T], ALU.add),
                                         reads=[bLb[k]], writes=[bRb[hh]])

                            def S3(n):
                                hh, step, sb, diag, col0 = info(n)
                                k = (itc + n) % NR
                                ba = state.pop(n)
                                P.op("act", lambda e: e.activation(Abuf[k][:, col0:T], ps[:, ba, col0:T], AF.Exp, scale=-1.0),
                                     reads=[pbuf[ba]], writes=[bAb[k]])
                                release(ba)
                                mm(ps[:, bo, col0:T], vzc[:, sb, hh, :], Abuf[k][:, col0:T], n == 0, n == nit - 1,
                                   [bvzc[sb // 4], bAb[k]], [pbuf[bo]], signal=True)

                            for n in range(nit + 2):
                                if n < nit:
                                    S1a(n)
                                if 0 <= n - 1 < nit:
                                    S2(n - 1)
                                if 0 <= n - 2 < nit:
                                    S3(n - 2)
                                if n < nit:
                                    S1b(n)
                            itc += nit
                            tok = slice(ti * T, (ti + 1) * T)
                            P.op("dve", lambda e, bo=bo, j=j, tok=tok: e.tensor_copy(yc[:, j, tok], ps[:, bo, :]), reads=[pbuf[bo]], writes=[byc[j][ti]])
                            release(bo)
                    scr_users = keep + mineC

                scr.off = scr_mark
                P.barrier()
                if "E" in parts:
                    merged = scr.bf(8 * T).rearrange("p (c t) -> p c t", c=8); bmer = [Buf() for _ in range(8)]
                    obuf = scr.f32(8 * T).rearrange("p (c t) -> p c t", c=8); bob = [Buf() for _ in range(8)]
                    uT = scr.bf(4 * T).rearrange("p (c t) -> p c t", c=4); buT = [Buf() for _ in range(4)]
                    yaT = scr.bf(4 * T).rearrange("p (c t) -> p c t", c=4); byaT = Buf()
                    vtmp = scr.f32(T); bvt = Buf()
                    vln = scr.bf(T); bvln = Buf()
                    svt = vtmp; bsvt = bvt
                    gsb = [scr.f32(T) for _ in range(3)]; bgsb = [Buf() for _ in range(3)]
                    sqe = [scr.bf(T) for _ in range(2)]; bsqe = [Buf() for _ in range(2)]
                    rstd = gsb[2]; brstd = bgsb[2]
                    tmp = gsb[0:2]; btmp = bgsb[0:2]
                    st6 = scr.f32(8); bst6 = Buf()
                    mv = scr.f32(4); bmv = Buf()
                    mineE = new_scratch(bmer + bob + buT + [byaT, bvt, bvln] + bgsb + bsqe + [bst6, bmv])
                    lng = lnrep[:, 0:512]
                    lnb = lnrep[:, 512:1024]
                    wpa = wview(w_pa[l]); wpb = wview(w_pb[l]); wpc = wview(w_pc[l]); wo_ = wview(w_o[l])
                    for ti in range(NT):
                        tok = slice(ti * T, (ti + 1) * T)
                        for up in range(2):
                            us, bus = W.get(win[:, :, up * 256:(up + 1) * 256], 8, 256)
                            for uu in range(2):
                                c = up * 2 + uu
                                b = bank()
                                for kc in range(8):
                                    mm(ps[:, b, :], us[:, kc, uu * 128:(uu + 1) * 128], hT[:, kc, tok], kc == 0, kc == 7,
                                       [bus, hbuf[ti]], [pbuf[b]], signal=(kc == 7))
                                P.op("act", lambda e, c=c, b=b: e.activation(uT[:, c, :], ps[:, b, :], AF.Gelu_apprx_tanh),
                                     reads=[pbuf[b]], writes=[buT[c]])
                                release(b)
                        vs = W.get_many([(win[:, :, 512 + hf * 256: 512 + (hf + 1) * 256], 8, 256) for hf in range(2)])
                        for blk in range(4):
                            tb = slice(ti * T + blk * 128, ti * T + (blk + 1) * 128)
                            b = bank()
                            for hf in range(2):
                                for kc in range(8):
                                    mm(ps[:, b, hf * 256:(hf + 1) * 256], hT[:, kc, tb], vs[hf][0][:, kc, :], kc == 0, kc == 7,
                                       [vs[hf][1], hbuf[ti]], [pbuf[b]], signal=(kc == 7))
                            P.op("act", lambda e, b=b: e.activation(vtmp, ps[:, b, :], AF.Gelu_apprx_tanh), reads=[pbuf[b]], writes=[bvt])
                            release(b)
                            P.op("dve", lambda e: e.bn_stats(st6[:, 0:6], vtmp), reads=[bvt], writes=[bst6])
                            P.op("dve", lambda e: e.bn_aggr(mv[:, 0:2], st6[:, 0:6]), reads=[bst6], writes=[bmv])
                            P.op("act", lambda e: e.activation(mv[:, 2:3], mv[:, 1:2], AF.Sqrt, bias=eps_ap), reads=[bmv, bcf], writes=[bmv])
                            P.op("dve", lambda e: e.reciprocal(mv[:, 2:3], mv[:, 2:3]), reads=[bmv], writes=[bmv])
                            P.op("dve", lambda e: e.tensor_scalar(vtmp, vtmp, mv[:, 0:1], mv[:, 2:3], ALU.subtract, ALU.mult),
                                 reads=[bmv, bvt], writes=[bvt])
                            P.op("pool", lambda e: e.tensor_tensor(vtmp, vtmp, lng, ALU.mult), reads=[bvt, blnrep], writes=[bvt])
                            P.op("pool", lambda e: e.tensor_tensor(vln, vtmp, lnb, ALU.add), reads=[bvt, blnrep], writes=[bvln])
                            b = bank()
                            for g in range(4):
                                mm(ps[:, b, g * 128:(g + 1) * 128], vln[:, g * 128:(g + 1) * 128], wsT[:, g * 128:(g + 1) * 128], True, True,
                                   [bvln, bwsT], [pbuf[b]], signal=(g == 3))
                            P.op("dve", lambda e, b=b: e.tensor_tensor(svt, ps[:, b, :], bsrep, ALU.add), reads=[pbuf[b], bbsrep], writes=[bsvt])
                            release(b)
                            P.op("dve", lambda e, blk=blk: e.tensor_tensor(yaT[:, :, blk * 128:(blk + 1) * 128],
                                                                            svt.rearrange("p (g t) -> p g t", g=4),
                                                                            uT[:, :, blk * 128:(blk + 1) * 128], ALU.mult),
                                 reads=[bsvt] + buT, writes=[byaT])
                        for m in range(8):
                            mc = slice(m * 128, (m + 1) * 128)
                            gb_ = []
                            for br in range(3):
                                gsl, bgsl = W.get(win[:, :, GATE0 + br * 1024 + m * 128: GATE0 + br * 1024 + (m + 1) * 128], 8, 128)
                                b = bank()
                                for kc in range(8):
                                    mm(ps[:, b, :], gsl[:, kc, :], hT[:, kc, tok], kc == 0, kc == 7, [bgsl, hbuf[ti]], [pbuf[b]], signal=(kc == 7))
                                P.op("act", lambda e, br=br, b=b: e.activation(gsb[br], ps[:, b, :], AF.Sigmoid), reads=[pbuf[b]], writes=[bgsb[br]])
                                release(b)
                            srcs = [(wpa, 4, yaT, lambda kc: [byaT]), (wpb, 2, yb, lambda kc: [byb[kc]]), (wpc, 4, yc, lambda kc: [byc[kc][ti]])]
                            for br, (wv, nk, ysrc, bf_) in enumerate(srcs):
                                psl, bpsl = W.get(wv[:, :, mc], nk, 128)
                                b = bank()
                                for kc in range(nk):
                                    rhs = ysrc[:, kc, :] if br == 0 else ysrc[:, kc, tok]
                                    mm(ps[:, b, :], psl[:, kc, :], rhs, kc == 0, kc == nk - 1, [bpsl] + bf_(kc), [pbuf[b]], signal=(kc == nk - 1))
                                P.op("dve", lambda e, br=br, b=b: e.tensor_tensor(gsb[br], gsb[br], ps[:, b, :], ALU.mult),
                                     reads=[pbuf[b], bgsb[br]], writes=[bgsb[br]])
                                release(b)
                            P.op("pool", lambda e: e.tensor_tensor(gsb[0], gsb[0], gsb[1], ALU.add), reads=[bgsb[0], bgsb[1]], writes=[bgsb[0]])
                            P.op("pool", lambda e, m=m: e.tensor_tensor(merged[:, m, :], gsb[0], gsb[2], ALU.add), reads=[bgsb[0], bgsb[2]], writes=[bmer[m]])
                        bss = bank()
                        for m in range(8):
                            osl, bosl = W.get(wo_[:, :, m * 128:(m + 1) * 128], 8, 128)
                            b = bank()
                            for kc in range(8):
                                mm(ps[:, b, :], osl[:, kc, :], merged[:, kc, :], kc == 0, kc == 7, [bosl, bmer[kc]], [pbuf[b]], signal=(kc == 7))
                            P.op("act", lambda e, m=m, b=b: e.activation(obuf[:, m, :], ps[:, b, :], AF.Copy), reads=[pbuf[b]], writes=[bob[m]])
                            k = m % 2
                            P.op("act", lambda e, k=k, b=b: e.activation(sqe[k], ps[:, b, :], AF.Square), reads=[pbuf[b]], writes=[bsqe[k]])
                            release(b)
                            if m > 0:
                                mm(ps[:, bss, :], ones, sqe[(m - 1) % 2], m == 1, False, [bsqe[(m - 1) % 2], bcbf], [pbuf[bss]])
                        mm(ps[:, bss, :], ones, sqe[7 % 2], False, True, [bsqe[7 % 2], bcbf], [pbuf[bss]])
                        postnorm_residual(l, G_MPOST, ti, obuf, bob, bss, rstd, brstd, tmp, btmp, False)
                    scr_users = keep + mineE

            plan_ = plan
            if plan_ is None:
                plan_ = []
                for l in range(DEPTH):
                    plan_ += [("ffn1", l), ("mix", l), ("ffn2", l)]
            for name, l in plan_:
                if name == "ffn1":
                    ffn(l, 0)
                elif name == "ffn2":
                    ffn(l, 1)
                else:
                    mixer(l)
                if stop == "%s_%d" % (name, l):
                    break
            outv = out.rearrange("(c p) t -> p c t", p=128)
            if dump is not None:
                src, bufs, nch = dump_src[dump]
                P.dma("pool", outv[:, 0:nch, :], src, reads=bufs, dbuf=bufs[0])
                if not P.dry:
                    P._need("sp", (bufs[0].dsem, bufs[0].dcnt))
                return
            for ti in range(NT):
                P.dma("sp", outv[:, :, ti * T:(ti + 1) * T], xT[:, :, ti * T:(ti + 1) * T],
                      reads=[xbuf[c][ti] for c in range(8)], dbuf=xbuf[0][ti])
            for ti in range(NT):
                P._need("sp", (xbuf[0][ti].dsem, xbuf[0][ti].dcnt)) if not P.dry else None

        P.dry = True
        body()
        P.dry = False
        body()
        assert W.i == len(W.plan)
        P.emit()
        print("instr counts:", {e: len(P.q[e]) for e in ENGINES}, "waits", P.nwait, "dma sems", P.ndsem, "scratch cap", scr_cap)
    return nc


def _consts():
    i = np.arange(128)
    r = i[:, None]
    c = i[None, :]
    ident = (r == c).astype(np.float32)
    tri = (r >= c).astype(np.float32)
    ones = np.ones((128, 128), np.float32)
    onesL = np.concatenate([np.ones((128, 64)), np.zeros((128, 64))], 1).astype(np.float32)
    onesR = np.concatenate([np.zeros((128, 64)), np.ones((128, 64))], 1).astype(np.float32)
    Md = np.where(r <= c, 0.0, NEG).astype(np.float32)
    Mp = np.where(c <= r, 0.0, NEG).astype(np.float32)
    mrow = np.concatenate([Mp, Mp, Md, Md], 1)
    zmask = np.where(r >= c, NEG, 0.0).astype(np.float32)
    amask = np.where(r >= c, -NEG, 0.0).astype(np.float32)
    cbf = np.concatenate([ident, tri, ones, onesL, onesR, mrow, zmask, amask, np.zeros((128, 128), np.float32)], 1)
    assert cbf.shape == (128, 1536)
    maskA = (c >= r).astype(np.float32)
    cf = np.zeros((128, 132), np.float32)
    cf[:, 0:128] = maskA
    cf[:, 128] = EPS
    cf[:, 129] = 1.0
    return cbf.astype(ml_dtypes.bfloat16), cf


_NC_CACHE = {}


def _get_nc(stop=None):
    if stop not in _NC_CACHE:
        _NC_CACHE[stop] = build_program(stop=stop)
    return _NC_CACHE[stop]


def make_in_maps(inputs, n_cores=8):
    f = lambda a: np.ascontiguousarray(np.asarray(a, dtype=np.float32))
    x = f(inputs["x"])
    gl = []
    for name in ["ffn1_pre_g", "ffn1_post_g", "mix_pre_g", "mix_post_g", "ffn2_pre_g", "ffn2_post_g"]:
        g = f(inputs[name])
        gl.append(g.reshape(DEPTH, 8, 128).transpose(2, 0, 1).reshape(128, 16))
    gains = np.ascontiguousarray(np.concatenate(gl, 1))
    cbf, cf = _consts()
    ln_g = f(inputs["a_ln_g"]); ln_b = f(inputs["a_ln_b"])
    lnrep = np.ascontiguousarray(np.concatenate([
        np.broadcast_to(ln_g[:, None, :], (DEPTH, 128, 512)),
        np.broadcast_to(ln_b[:, None, :], (DEPTH, 128, 512))], 2))
    bs = f(inputs["a_bs"])
    bsrep = np.ascontiguousarray(np.broadcast_to(bs.reshape(DEPTH, 1, 512), (DEPTH, 128, 512)))
    ws = f(inputs["a_ws"])
    wsT = np.ascontiguousarray(ws.transpose(0, 3, 1, 2).reshape(DEPTH, 128, 512))
    shared = {
        "gains": gains, "cbf": cbf, "cf32": cf, "lnrep": lnrep, "bsrep": bsrep, "wsT": wsT,
        "ffn1_wi": f(inputs["ffn1_wi"]), "ffn2_wi": f(inputs["ffn2_wi"]),
        "ffn1_wo": f(inputs["ffn1_wo"]), "ffn2_wo": f(inputs["ffn2_wo"]),
        "w_in": f(inputs["w_in"]), "w_pa": f(inputs["w_pa"]), "w_pb": f(inputs["w_pb"]),
        "w_pc": f(inputs["w_pc"]), "w_o": f(inputs["w_o"]),
    }
    maps = []
    for b in range(n_cores):
        m = dict(shared)
        m["xT"] = np.ascontiguousarray(x[b].T)
        maps.append(m)
    return maps


def kernel(**inputs):
    nc = _get_nc()
    in_maps = make_in_maps(inputs, 8)
    res = run_bass_kernel_spmd(nc, in_maps, core_ids=list(range(8)))
    outs = [np.asarray(r["outT"]).T for r in res.results]
    return np.ascontiguousarray(np.stack(outs, 0).astype(np.float32))
```
